# Optimizing a Trainium2 kernel written in Bass

```python
import math
import jax, jax.numpy as jnp
from jax import lax
import numpy as np

D_MODEL = 1024
BATCH = 32
SEQ = 2048
DEPTH = 4

GRID_W = 64
CTX_LEN = 256
N_MIXERS = 3
N_MOD = 9
FFN_HIDDEN = 256 * ((8 * D_MODEL + 767) // 768)
FFN_RESIDUAL = 0.5
DN_HEAD_DIM = 128
DN_HEADS = D_MODEL // DN_HEAD_DIM
DN_CONV = 5
DN_CHUNK = 64
SWA_HEAD_DIM = 64
SWA_HEADS = D_MODEL // SWA_HEAD_DIM
SWA_KV_HEADS = SWA_HEADS // 4
WINDOW = 128
ATTN_BLOCK = 128
RET_HEADS = 4
RET_QK_DIM = D_MODEL // RET_HEADS
RET_V_DIM = 2 * RET_QK_DIM
RET_CHUNK = 128
ROPE_BASE = 10000.0
EPS = 1e-6

kernel_name = "hybrid_deltanet_swa_retention_dit"


def _rms_norm(x, gain):
    xf = x.astype(jnp.float32)
    y = xf * lax.rsqrt(jnp.mean(xf * xf, axis=-1, keepdims=True) + EPS)
    return (y * gain.astype(jnp.float32)).astype(x.dtype)


def _head_group_norm(x, gain):
    xf = x.astype(jnp.float32)
    mu = jnp.mean(xf, axis=-1, keepdims=True)
    var = jnp.mean(jnp.square(xf - mu), axis=-1, keepdims=True)
    return ((xf - mu) * lax.rsqrt(var + EPS) * gain.astype(jnp.float32)).astype(x.dtype)


def _l2_normalize(x):
    xf = x.astype(jnp.float32)
    return (xf * lax.rsqrt(jnp.sum(xf * xf, axis=-1, keepdims=True) + EPS)).astype(x.dtype)


def _modulated_norm(h, gain, shift, scale):
    return _rms_norm(h, gain) * (1.0 + scale) + shift


def _swiglu(h, w1, w2):
    a, b = jnp.split(h @ w1, 2, axis=-1)
    return (jax.nn.silu(a) * b) @ w2


def _rotary_tables(positions, n_freq):
    inv = ROPE_BASE ** (-jnp.arange(n_freq, dtype=jnp.float32) / n_freq)
    ang = jnp.concatenate([p.astype(jnp.float32)[:, None] * inv[None, :] for p in positions], axis=-1)
    return jnp.cos(ang), jnp.sin(ang)


def _rope(x, cos, sin):
    x1, x2 = jnp.split(x, 2, axis=-1)
    c = cos[:, None, :].astype(x.dtype)
    s = sin[:, None, :].astype(x.dtype)
    return jnp.concatenate([x1 * c - x2 * s, x1 * s + x2 * c], axis=-1)


def _short_conv(x, w):
    k = w.shape[0]
    return lax.conv_general_dilated(x, w[:, None, :], window_strides=(1,), padding=[(k // 2, k // 2)],
                                    dimension_numbers=('NWC', 'WIO', 'NWC'), feature_group_count=x.shape[-1])


def _join(a_ctx, a_lat, reverse):
    if reverse:
        a_ctx, a_lat = jnp.flip(a_ctx, 1), jnp.flip(a_lat, 1)
    return jnp.concatenate([a_ctx, a_lat], axis=1)


def _both_dirs(a_ctx, a_lat):
    return jnp.concatenate([_join(a_ctx, a_lat, False), _join(a_ctx, a_lat, True)], axis=0)


def _split_dirs(o, n_batch, ctx_len):
    of, ob = o[:n_batch], o[n_batch:]
    return (of[:, :ctx_len], of[:, ctx_len:], jnp.flip(ob[:, :ctx_len], 1), jnp.flip(ob[:, ctx_len:], 1))


def _to_chunks(a, size):
    n, t, h = a.shape[:3]
    a = a.astype(jnp.float32).reshape((n, t // size, size, h) + a.shape[3:])
    return jnp.moveaxis(a, 3, 1)


def _from_chunks(a):
    n, h, nc, c = a.shape[:4]
    return jnp.moveaxis(a, 1, 3).reshape((n, nc * c, h) + a.shape[4:])


def _gated_delta_chunked(q, k, v, g, beta):
    out_dtype = v.dtype
    n, t, h, dk = q.shape
    dv = v.shape[-1]
    q = _to_chunks(q, DN_CHUNK) * dk ** -0.5
    k = _to_chunks(k, DN_CHUNK)
    v = _to_chunks(v, DN_CHUNK)
    gc = jnp.cumsum(_to_chunks(g, DN_CHUNK), axis=-1)
    beta = _to_chunks(beta, DN_CHUNK)
    causal = jnp.tril(jnp.ones((DN_CHUNK, DN_CHUNK), bool))
    strict = jnp.tril(jnp.ones((DN_CHUNK, DN_CHUNK), bool), -1)
    decay = jnp.exp(jnp.where(causal, gc[..., :, None] - gc[..., None, :], -jnp.inf))
    kb = k * beta[..., None]
    lower = jnp.where(strict, jnp.einsum('nhcid,nhcjd->nhcij', kb, k) * decay, 0.0)
    rhs = jnp.concatenate([v * beta[..., None], kb * jnp.exp(gc)[..., None]], axis=-1)
    uw = lax.linalg.triangular_solve(lower, rhs, left_side=True, lower=True, unit_diagonal=True)
    u, w = uw[..., :dv], uw[..., dv:]
    qk = jnp.einsum('nhcid,nhcjd->nhcij', q, k) * decay
    qg = q * jnp.exp(gc)[..., None]
    g_last = gc[..., -1]
    kd = k * jnp.exp(g_last[..., None] - gc)[..., None]
    xs = tuple(jnp.moveaxis(a, 2, 0) for a in (u, w, qk, qg, kd, jnp.exp(g_last)))

    def step(state, inp):
        u_c, w_c, qk_c, qg_c, kd_c, a_c = inp
        v_new = u_c - jnp.einsum('nhik,nhkv->nhiv', w_c, state)
        o = jnp.einsum('nhik,nhkv->nhiv', qg_c, state) + jnp.einsum('nhij,nhjv->nhiv', qk_c, v_new)
        state = state * a_c[..., None, None] + jnp.einsum('nhjk,nhjv->nhkv', kd_c, v_new)
        return state, o

    _, o = lax.scan(step, jnp.zeros((n, h, dk, dv), jnp.float32), xs)
    return _from_chunks(jnp.moveaxis(o, 0, 2)).astype(out_dtype)


def _retention_log_decay():
    return jnp.log1p(-jnp.exp2(-5.0 - jnp.arange(RET_HEADS, dtype=jnp.float32)))


def _retention_chunked(q, k, v, log_gamma):
    out_dtype = v.dtype
    n, t, h, dk = q.shape
    dv = v.shape[-1]
    q = _to_chunks(q, RET_CHUNK)
    k = _to_chunks(k, RET_CHUNK) * dk ** -0.5
    v = _to_chunks(v, RET_CHUNK)
    pos = jnp.arange(RET_CHUNK, dtype=jnp.float32)
    lg = log_gamma[:, None]
    rel = pos[:, None] - pos[None, :]
    decay = jnp.exp(jnp.where(rel >= 0, lg[:, :, None] * rel, -jnp.inf))
    scores = jnp.einsum('nhcid,nhcjd->nhcij', q, k) * decay[:, None]
    o_intra = jnp.einsum('nhcij,nhcje->nhcie', scores, v)
    q_in = q * jnp.exp(lg * (pos + 1.0))[:, None, :, None]
    k_st = k * jnp.exp(lg * (RET_CHUNK - 1.0 - pos))[:, None, :, None]
    chunk_decay = jnp.exp(lg * RET_CHUNK)[:, :, None]

    def step(state, inp):
        q_c, k_c, v_c = inp
        o = jnp.einsum('nhid,nhde->nhie', q_c, state)
        state = state * chunk_decay + jnp.einsum('nhjd,nhje->nhde', k_c, v_c)
        return state, o

    xs = tuple(jnp.moveaxis(a, 2, 0) for a in (q_in, k_st, v))
    _, o_inter = lax.scan(step, jnp.zeros((n, h, dk, dv), jnp.float32), xs)
    return _from_chunks(o_intra + jnp.moveaxis(o_inter, 0, 2)).astype(out_dtype)


def _sink_softmax(logits, sink):
    sink_col = jnp.broadcast_to(sink.astype(jnp.float32)[None, :, :, None, None], logits.shape[:-1] + (1,))
    return jax.nn.softmax(jnp.concatenate([sink_col, logits], axis=-1), axis=-1)[..., 1:]


def _deltanet_mixer(hc, hx, w_in, conv_w, a_log, dt_bias, o_gain, w_out):
    n_batch, ctx_len = hx.shape[0], hc.shape[1]
    qkv_w = 3 * DN_HEADS * DN_HEAD_DIM
    z_w = DN_HEADS * DN_HEAD_DIM

    def project(h):
        n = h.shape[1]
        qkv, z, ab = jnp.split(h @ w_in, [qkv_w, qkv_w + z_w], axis=-1)
        qkv = jax.nn.silu(_short_conv(qkv, conv_w))
        q, k, v = [a.reshape(n_batch, n, DN_HEADS, DN_HEAD_DIM) for a in jnp.split(qkv, 3, axis=-1)]
        ab = ab.astype(jnp.float32).reshape(n_batch, n, 2, 2, DN_HEADS)
        g = -jnp.exp(a_log.astype(jnp.float32)) * jax.nn.softplus(ab[:, :, :, 0] + dt_bias.astype(jnp.float32))
        beta = jax.nn.sigmoid(ab[:, :, :, 1])
        return _l2_normalize(q), _l2_normalize(k), v, z.reshape(n_batch, n, DN_HEADS, DN_HEAD_DIM), g, beta

    qc, kc, vc, zc, gc, bc = project(hc)
    qx, kx, vx, zx, gx, bx = project(hx)
    g_all = jnp.concatenate([_join(gc[:, :, 0], gx[:, :, 0], False), _join(gc[:, :, 1], gx[:, :, 1], True)], axis=0)
    b_all = jnp.concatenate([_join(bc[:, :, 0], bx[:, :, 0], False), _join(bc[:, :, 1], bx[:, :, 1], True)], axis=0)
    o = _gated_delta_chunked(_both_dirs(qc, qx), _both_dirs(kc, kx), _both_dirs(vc, vx), g_all, b_all)
    ofc, ofx, obc, obx = _split_dirs(o, n_batch, ctx_len)

    def finish(o_f, o_b, z):
        y = _rms_norm(o_f + o_b, o_gain) * jax.nn.silu(z)
        return y.reshape(y.shape[0], y.shape[1], -1) @ w_out

    return finish(ofc, obc, zc), finish(ofx, obx, zx)


def _window_attention_mixer(hc, hx, w_qkv, q_gain, k_gain, sink, w_out, cos, sin):
    n_batch, n_tok = hx.shape[0], hx.shape[1]
    ctx_len = hc.shape[1]
    group = SWA_HEADS // SWA_KV_HEADS
    q_w = SWA_HEADS * SWA_HEAD_DIM
    kv_w = SWA_KV_HEADS * SWA_HEAD_DIM
    scale = SWA_HEAD_DIM ** -0.5
    sink_hg = sink.reshape(SWA_KV_HEADS, group)

    def project(h, rotate):
        n = h.shape[1]
        q, k, v = jnp.split(h @ w_qkv, [q_w, q_w + kv_w], axis=-1)
        q = _rms_norm(q.reshape(n_batch, n, SWA_HEADS, SWA_HEAD_DIM), q_gain)
        k = _rms_norm(k.reshape(n_batch, n, SWA_KV_HEADS, SWA_HEAD_DIM), k_gain)
        if rotate:
            q, k = _rope(q, cos, sin), _rope(k, cos, sin)
        q = (q * scale).reshape(n_batch, n, SWA_KV_HEADS, group, SWA_HEAD_DIM)
        return q, k, v.reshape(n_batch, n, SWA_KV_HEADS, SWA_HEAD_DIM)

    qc, kc, vc = project(hc, False)
    qx, kx, vx = project(hx, True)

    s_c = jnp.einsum('bqhgd,bkhd->bhgqk', qc, kc).astype(jnp.float32)
    p_c = _sink_softmax(s_c, sink_hg).astype(vc.dtype)
    oc = jnp.einsum('bhgqk,bkhd->bqhgd', p_c, vc).reshape(n_batch, ctx_len, q_w)

    span = ATTN_BLOCK + 2 * WINDOW
    pad = ((0, 0), (WINDOW, WINDOW), (0, 0), (0, 0))
    kp, vp = jnp.pad(kx, pad), jnp.pad(vx, pad)
    q_off = jnp.arange(ATTN_BLOCK)
    k_off = jnp.arange(span)

    def block(j):
        start = j * ATTN_BLOCK
        qb = lax.dynamic_slice_in_dim(qx, start, ATTN_BLOCK, axis=1)
        kb = lax.dynamic_slice_in_dim(kp, start, span, axis=1)
        vb = lax.dynamic_slice_in_dim(vp, start, span, axis=1)
        t_pos = start + q_off
        s_pos = start - WINDOW + k_off
        allowed = (jnp.abs(t_pos[:, None] - s_pos[None, :]) <= WINDOW) & (s_pos >= 0) & (s_pos < n_tok)
        s_ctx = jnp.einsum('bqhgd,bkhd->bhgqk', qb, kc).astype(jnp.float32)
        s_win = jnp.where(allowed, jnp.einsum('bqhgd,bkhd->bhgqk', qb, kb).astype(jnp.float32), -jnp.inf)
        p = _sink_softmax(jnp.concatenate([s_ctx, s_win], axis=-1), sink_hg).astype(vb.dtype)
        return (jnp.einsum('bhgqk,bkhd->bqhgd', p[..., :ctx_len], vc)
                + jnp.einsum('bhgqk,bkhd->bqhgd', p[..., ctx_len:], vb))

    ox = lax.map(block, jnp.arange(n_tok // ATTN_BLOCK))
    ox = jnp.moveaxis(ox, 0, 1).reshape(n_batch, n_tok, q_w)
    return oc @ w_out, ox @ w_out


def _retention_mixer(hc, hx, w_in, gn_gain, w_out, cos, sin):
    n_batch, ctx_len = hx.shape[0], hc.shape[1]
    qk_w = RET_HEADS * RET_QK_DIM
    v_w = RET_HEADS * RET_V_DIM

    def project(h):
        n = h.shape[1]
        q, k, v, gf, gb = jnp.split(h @ w_in, [qk_w, 2 * qk_w, 2 * qk_w + v_w, 2 * qk_w + 2 * v_w], axis=-1)
        hq = lambda a: a.reshape(n_batch, n, RET_HEADS, RET_QK_DIM)
        hv = lambda a: a.reshape(n_batch, n, RET_HEADS, RET_V_DIM)
        return hq(q), hq(k), hv(v), hv(gf), hv(gb)

    qc, kc, vc, gfc, gbc = project(hc)
    qx, kx, vx, gfx, gbx = project(hx)
    qx, kx = _rope(qx, cos, sin), _rope(kx, cos, sin)
    o = _retention_chunked(_both_dirs(qc, qx), _both_dirs(kc, kx), _both_dirs(vc, vx), _retention_log_decay())
    ofc, ofx, obc, obx = _split_dirs(o, n_batch, ctx_len)
    gain = gn_gain.reshape(2, RET_HEADS, RET_V_DIM)

    def finish(o_f, o_b, g_f, g_b):
        y = (jax.nn.silu(g_f) * _head_group_norm(o_f, gain[0]) + jax.nn.silu(g_b) * _head_group_norm(o_b, gain[1]))
        return y.reshape(y.shape[0], y.shape[1], -1) @ w_out

    return finish(ofc, obc, gfc, gbc), finish(ofx, obx, gfx, gbx)


def setup_inputs(seed: int = 0) -> dict:
    key = jax.random.key(seed)
    ks = jax.random.split(key, 24)
    d, f = D_MODEL, FFN_HIDDEN
    n_dn, n_swa, n_ret = (len(range(kind, DEPTH, N_MIXERS)) for kind in range(N_MIXERS))

    def nrm(k, shape, std):
        return std * jax.random.normal(k, shape, jnp.float32)

    dn_in = 4 * DN_HEADS * DN_HEAD_DIM + 4 * DN_HEADS
    swa_in = (SWA_HEADS + 2 * SWA_KV_HEADS) * SWA_HEAD_DIM
    ret_v = RET_HEADS * RET_V_DIM
    ret_in = 2 * RET_HEADS * RET_QK_DIM + 3 * ret_v
    dt = jnp.exp(jax.random.uniform(ks[12], (n_dn, 2, DN_HEADS), jnp.float32, math.log(1e-3), math.log(1e-1)))
    return {
        "x": nrm(ks[0], (BATCH, SEQ, d), 1.0),
        "c": nrm(ks[1], (BATCH, d), 1.0),
        "ctx": nrm(ks[2], (BATCH, CTX_LEN, d), 1.0),
        "c_ctx": nrm(ks[3], (d,), 1.0),
        "ada_w": nrm(ks[4], (DEPTH, d, N_MOD * d), 0.5 * d ** -0.5),
        "ada_b": nrm(ks[5], (DEPTH, N_MOD * d), 0.02),
        "norm_g": 1.0 + nrm(ks[6], (DEPTH, 3, d), 0.02),
        "ffn_w1": nrm(ks[7], (DEPTH, 2, d, 2 * f), d ** -0.5),
        "ffn_w2": nrm(ks[8], (DEPTH, 2, f, d), f ** -0.5),
        "dn_w_in": nrm(ks[9], (n_dn, d, dn_in), d ** -0.5),
        "dn_conv": nrm(ks[10], (n_dn, DN_CONV, 3 * DN_HEADS * DN_HEAD_DIM), DN_CONV ** -0.5),
        "dn_a_log": jnp.log(jax.random.uniform(ks[11], (n_dn, 2, DN_HEADS), jnp.float32, 1.0, 16.0)),
        "dn_dt_bias": dt + jnp.log(-jnp.expm1(-dt)),
        "dn_o_gain": 1.0 + nrm(ks[13], (n_dn, DN_HEAD_DIM), 0.02),
        "dn_w_out": nrm(ks[14], (n_dn, DN_HEADS * DN_HEAD_DIM, d), (DN_HEADS * DN_HEAD_DIM) ** -0.5),
        "swa_w_qkv": nrm(ks[15], (n_swa, d, swa_in), d ** -0.5),
        "swa_q_gain": 1.0 + nrm(ks[16], (n_swa, SWA_HEAD_DIM), 0.02),
        "swa_k_gain": 1.0 + nrm(ks[17], (n_swa, SWA_HEAD_DIM), 0.02),
        "swa_sink": nrm(ks[18], (n_swa, SWA_HEADS), 1.0),
        "swa_w_out": nrm(ks[19], (n_swa, SWA_HEADS * SWA_HEAD_DIM, d), (SWA_HEADS * SWA_HEAD_DIM) ** -0.5),
        "ret_w_in": nrm(ks[20], (n_ret, d, ret_in), d ** -0.5),
        "ret_gn_gain": 1.0 + nrm(ks[21], (n_ret, 2, ret_v), 0.02),
        "ret_w_out": nrm(ks[22], (n_ret, ret_v, d), ret_v ** -0.5),
    }


def reference(x, c, ctx, c_ctx, ada_w, ada_b, norm_g, ffn_w1, ffn_w2, dn_w_in, dn_conv, dn_a_log, dn_dt_bias,
              dn_o_gain, dn_w_out, swa_w_qkv, swa_q_gain, swa_k_gain, swa_sink, swa_w_out, ret_w_in, ret_gn_gain,
              ret_w_out):
    n_batch, n_tok, d = x.shape
    ROWS = n_tok // GRID_W
    rows = jnp.repeat(jnp.arange(ROWS), GRID_W)
    cols = jnp.tile(jnp.arange(GRID_W), ROWS)
    swa_cos, swa_sin = _rotary_tables((rows, cols), SWA_HEAD_DIM // 4)
    ret_cos, ret_sin = _rotary_tables((jnp.arange(n_tok),), RET_QK_DIM // 2)
    cond_x = jax.nn.silu(c)
    cond_c = jax.nn.silu(c_ctx)

    for i in range(DEPTH):
        kind, slot, last = i % N_MIXERS, i // N_MIXERS, i == DEPTH - 1
        mx = (cond_x @ ada_w[i] + ada_b[i]).reshape(n_batch, 3, 3, 1, d)
        mc = (cond_c @ ada_w[i] + ada_b[i]).reshape(3, 3, d)

        x = x + FFN_RESIDUAL * mx[:, 0, 2] * _swiglu(
            _modulated_norm(x, norm_g[i, 0], mx[:, 0, 0], mx[:, 0, 1]), ffn_w1[i, 0], ffn_w2[i, 0])
        ctx = ctx + FFN_RESIDUAL * mc[0, 2] * _swiglu(
            _modulated_norm(ctx, norm_g[i, 0], mc[0, 0], mc[0, 1]), ffn_w1[i, 0], ffn_w2[i, 0])

        hx = _modulated_norm(x, norm_g[i, 1], mx[:, 1, 0], mx[:, 1, 1])
        hc = _modulated_norm(ctx, norm_g[i, 1], mc[1, 0], mc[1, 1])
        if kind == 0:
            oc, ox = _deltanet_mixer(hc, hx, dn_w_in[slot], dn_conv[slot], dn_a_log[slot], dn_dt_bias[slot],
                                     dn_o_gain[slot], dn_w_out[slot])
        elif kind == 1:
            oc, ox = _window_attention_mixer(hc, hx, swa_w_qkv[slot], swa_q_gain[slot], swa_k_gain[slot],
                                             swa_sink[slot], swa_w_out[slot], swa_cos, swa_sin)
        else:
            oc, ox = _retention_mixer(hc, hx, ret_w_in[slot], ret_gn_gain[slot], ret_w_out[slot], ret_cos, ret_sin)
        x = x + mx[:, 1, 2] * ox

        x = x + FFN_RESIDUAL * mx[:, 2, 2] * _swiglu(
            _modulated_norm(x, norm_g[i, 2], mx[:, 2, 0], mx[:, 2, 1]), ffn_w1[i, 1], ffn_w2[i, 1])
        if not last:
            ctx = ctx + mc[1, 2] * oc
            ctx = ctx + FFN_RESIDUAL * mc[2, 2] * _swiglu(
                _modulated_norm(ctx, norm_g[i, 2], mc[2, 0], mc[2, 1]), ffn_w1[i, 1], ffn_w2[i, 1])
    return x
```

```python
import numpy as np
from contextlib import ExitStack
import concourse.bass as bass
import concourse.mybir as mybir
from concourse.bass_utils import run_bass_kernel_spmd

F32 = mybir.dt.float32
BF16 = mybir.dt.bfloat16
AF = mybir.ActivationFunctionType
ALU = mybir.AluOpType
AX = mybir.AxisListType

D = 1024
NCH = 8
SEQ = 2048
CTX = 256
T = SEQ + CTX
DEPTH = 4
FFN_H = 2816
NFB = 22
EPS = 1e-6
NCORES = 8
TILES = [(0, 256, True)] + [(256 + 512 * i, 512, False) for i in range(4)]
FGROUPS = [[0, 1, 2, 3], [4, 5, 6, 7], [8, 9, 10, 11], [12, 13, 14, 15], [16, 17, 18], [19, 20, 21]]


C_ID = 0
C_LE = 128
C_GE = 256
C_NH = 384
C_ONE = 385
CST_W = 385 + 128
RET_CW = 514


def _make_consts():
    f = np.float32
    c = np.zeros((128, CST_W), f)
    m = np.arange(128)
    c[:, C_ID:C_ID + 128] = np.eye(128, dtype=f)
    c[:, C_LE:C_LE + 128] = (m[:, None] <= m[None, :]).astype(f)
    c[:, C_GE:C_GE + 128] = (m[:, None] >= m[None, :]).astype(f)
    c[:, C_NH] = -0.5
    c[:, C_ONE:C_ONE + 128] = 1.0
    return c


def _make_dn_masks():
    i = np.arange(128)
    I, J = np.meshgrid(i, i, indexing="ij")
    out = np.zeros((128, 2, 7, 128), np.float32)
    for lv in range(7):
        b = 1 << lv
        same2 = (I // (2 * b)) == (J // (2 * b))
        diff1 = (I // b) != (J // b)
        lo = same2 & diff1 & (I > J)
        out[:, 0, lv, :] = lo
        out[:, 1, lv, :] = lo.T
    return out


def _make_ret_consts():
    f = np.float64
    out = np.zeros((4, 128, RET_CW), np.float32)
    i = np.arange(128)
    for h in range(4):
        g = 1.0 - 2.0 ** (-5.0 - h)
        lg = np.log(g)
        J, I = np.meshgrid(i, i, indexing="ij")
        dec_f = np.where(I >= J, np.exp(lg * (I - J)), 0.0) / 16.0
        dec_b = np.where(J >= I, np.exp(lg * (J - I)), 0.0) / 16.0
        out[h, :, 0:128] = dec_f
        out[h, :, 128:256] = dec_b
        out[h, :, 256:384] = np.exp(lg * (i + 1.0))[None, :]
        out[h, :, 384:512] = np.exp(lg * (128.0 - i))[None, :]
        out[h, :, 512] = np.exp(lg * (127.0 - i)) / 16.0
        out[h, :, 513] = np.exp(lg * i) / 16.0
    return out


def _ret_gamma128(h):
    return float((1.0 - 2.0 ** (-5.0 - h)) ** 128)


class Buf:
    __slots__ = ("name", "lw", "rd", "excl")

    def __init__(self, name, excl=False):
        self.name = name
        self.lw = None
        self.rd = {}
        self.excl = excl


class Prog:
    def __init__(self, nc):
        self.nc = nc
        self.eng = {"pe": nc.tensor, "dve": nc.vector, "act": nc.scalar, "pool": nc.gpsimd, "sp": nc.sync}
        self.sem = {k: nc.alloc_semaphore("s_" + k) for k in ("pe", "dve", "act", "pool")}
        self.cnt = {k: 0 for k in self.sem}
        self.seen = {k: {} for k in self.eng}
        self.dsem = {}
        self.nbuf = 0

    def buf(self, name=None, excl=False):
        self.nbuf += 1
        return Buf(name or f"b{self.nbuf}", excl)

    def _deps(self, rd, wr):
        deps = []
        for b in rd:
            if b.lw is not None:
                deps.extend(b.lw)
            if b.excl:
                deps.extend(b.rd.values())
        for b in wr:
            if b.lw is not None:
                deps.extend(b.lw)
            deps.extend(b.rd.values())
        return deps

    def _wait(self, e, deps):
        need = {}
        for key, sem, val in deps:
            if key == e and e == "pe":
                continue
            if self.seen[e].get(key, 0) >= val:
                continue
            if key not in need or need[key][1] < val:
                need[key] = (sem, val)
        for key, (sem, val) in need.items():
            self.eng[e].wait_ge(sem, val)
            self.seen[e][key] = val

    def op(self, e, fn, rd=(), wr=()):
        self._wait(e, self._deps(rd, wr))
        ins = fn(self.eng[e])
        self.cnt[e] += 1
        ins.then_inc(self.sem[e], 1)
        tok = (e, self.sem[e], self.cnt[e])
        for b in rd:
            b.rd[e] = tok
        for b in wr:
            b.lw = [tok]
            b.rd = {}
        return ins

    def mm(self, fns, rd=(), wr=()):
        e = "pe"
        self._wait(e, self._deps(rd, wr))
        ins = None
        for f in fns:
            ins = f(self.eng[e])
        self.cnt[e] += 1
        ins.then_inc(self.sem[e], 1)
        tok = (e, self.sem[e], self.cnt[e])
        for b in rd:
            b.rd[e] = tok
        for b in wr:
            b.lw = [tok]
            b.rd = {}

    def dma(self, e, out_ap, in_ap, rd=(), wr=(), skey=None):
        if skey is None:
            skey = "d:" + e + ":" + (wr[0].name if wr else "out")
        if skey not in self.dsem:
            self.dsem[skey] = [self.nc.alloc_semaphore("s_" + skey.replace(":", "_")), 0]
        self._wait(e, self._deps(rd, wr))
        ins = self.eng[e].dma_start(out=out_ap, in_=in_ap)
        ds = self.dsem[skey]
        ds[1] += 16
        ins.then_inc(ds[0], 16)
        tok = (skey, ds[0], ds[1])
        for b in rd:
            b.rd[skey] = tok
        for b in wr:
            keep = [t for t in (b.lw or []) if t[0].startswith("d:") and t[0] != skey]
            b.lw = keep + [tok]
            b.rd = {}

    def barrier(self, engines=("pe", "dve", "act", "pool", "sp")):
        deps = [(k, self.sem[k], self.cnt[k]) for k in self.sem if self.cnt[k] > 0]
        deps += [(k, v[0], v[1]) for k, v in self.dsem.items() if v[1] > 0]
        for e in engines:
            self._wait(e, deps)

    def finish(self):
        deps = [(k, v[0], v[1]) for k, v in self.dsem.items() if v[1] > 0]
        deps += [(k, self.sem[k], self.cnt[k]) for k in self.sem if self.cnt[k] > 0]
        self._wait("sp", deps)


class Rot:
    def __init__(self, items):
        self.items = items
        self.i = 0

    def next(self):
        it = self.items[self.i % len(self.items)]
        self.i += 1
        return it


class Builder:
    def __init__(self, nseq, layers=(0, 1, 2, 3)):
        self.nseq = nseq
        self.layers = list(layers)
        self.li = {l: i for i, l in enumerate(self.layers)}
        NL = len(self.layers)
        nc = self.nc = bass.Bass("TRN2", target_bir_lowering=False)
        self.P = Prog(nc)
        P = self.P

        def din(name, shape, dt=F32):
            return nc.dram_tensor(name, list(shape), dt, kind="ExternalInput").ap()

        self.xT = din("xT", [nseq, 128, NCH, T])
        self.cT = din("cT", [128, NCH, 5])
        self.ada_w = din("ada_w", [NL, 72, 128, NCH, 128])
        self.ada_b = din("ada_b", [NL, 128, 72])
        self.norm_g = din("norm_g", [NL, 128, 24])
        self.w1 = din("ffn_w1", [NL, 2, 44, 128, NCH, 128])
        self.w2 = din("ffn_w2", [NL, 2, NCH, 128, NFB, 128])
        self.cst = din("cst", [128, CST_W])
        if 0 in self.li or 3 in self.li:
            self.dn_wqkv = din("dn_wqkv", [2, 24, 128, NCH, 128])
            self.dn_wz = din("dn_wz", [2, 8, 128, NCH, 128])
            self.dn_wab = din("dn_wab", [2, 128, NCH, 32])
            self.dn_wo = din("dn_wo", [2, 8, 128, D])
            self.dn_cw = din("dn_cw", [2, 128, 24, 5])
            self.dn_ab = din("dn_ab", [2, 128, 2, 16])
            self.dn_og = din("dn_og", [2, 128, 128])
            self.dn_masks = din("dn_masks", [128, 2, 7, 128])
        if 1 in self.li:
            self.swa_wqkv = din("swa_wqkv", [128, NCH, 1536])
            self.swa_wo = din("swa_wo", [128, NCH, D])
            self.swa_g = din("swa_g", [128, 2, 64])
            self.swa_sink = din("swa_sink", [128, 16])
            self.swa_cs = din("swa_cs", [SEQ, 2, 32])
        if 2 in self.li:
            self.ret_wqk = din("ret_wqk", [4, 4, 128, NCH, 128])
            self.ret_wvg = din("ret_wvg", [4, 3, 128, NCH, 512])
            self.ret_wo = din("ret_wo", [4, 128, 4, D])
            self.ret_gain = din("ret_gain", [128, 2, 2048])
            self.ret_cs = din("ret_cs", [2, 128, SEQ])
            self.ret_c = din("ret_c", [4, 128, RET_CW])
        self.outT = nc.dram_tensor("outT", [nseq, 128, NCH, SEQ], F32, kind="ExternalOutput").ap()

        self.XT = None
        self.XTb = None
        self.xs = nc.dram_tensor("xs", [128, NCH, T], F32, kind="Internal").ap()
        self.XSb = [P.buf(f"XS{i}") for i in range(len(TILES))]
        self.HT = nc.alloc_sbuf_tensor("HT", [128, NCH, T], BF16)
        self.HTb = [P.buf(f"HT{i}") for i in range(len(TILES))]
        self.MOD = nc.alloc_sbuf_tensor("MOD", [128, DEPTH, 72, 5], F32)
        self.MODb = P.buf("MOD")
        self.A1 = nc.alloc_sbuf_tensor("A1", [128, DEPTH, 3, NCH, 5], F32)
        self.GT = nc.alloc_sbuf_tensor("GT", [128, DEPTH, 3, NCH, 5], F32)
        self.NG = nc.alloc_sbuf_tensor("NG", [128, DEPTH, 24], F32)
        self.ones_bf = nc.alloc_sbuf_tensor("ones_bf", [128, 128], BF16)
        self.CONSTb = P.buf("CONST")
        self.PS = [nc.alloc_psum_tensor(f"ps{i}", [128, 512], F32) for i in range(6)]
        self.PSb = [P.buf(f"ps{i}", excl=True) for i in range(6)]
        self.PT = nc.alloc_psum_tensor("pst", [128, 1024], BF16)
        self.PTb = P.buf("pst", excl=True)
        self.PT2 = nc.alloc_psum_tensor("pst2", [128, 1024], BF16)
        self.CST = nc.alloc_sbuf_tensor("CST", [128, CST_W], F32)
        self.ident_b = nc.alloc_sbuf_tensor("ident_b", [128, 128], BF16)

    def U(self, name):
        self._uid = getattr(self, "_uid", 0) + 1
        return f"{name}_{self._uid}"

    def prologue(self):
        nc, P = self.nc, self.P
        with nc.sbuf_tensor(self.U("cTs"), [128, NCH, 5], F32) as cTs, \
                nc.sbuf_tensor(self.U("cTb"), [128, NCH, 5], BF16) as cTb, \
                nc.sbuf_tensor(self.U("adab"), [128, DEPTH, 72], F32) as adab, \
                nc.sbuf_tensor(self.U("sig"), [128, NCH, 5], F32) as sig, \
                nc.sbuf_tensor(self.U("adaw0"), [128, NCH, 128], BF16) as aw0, \
                nc.sbuf_tensor(self.U("adaw1"), [128, NCH, 128], BF16) as aw1, \
                nc.sbuf_tensor(self.U("adaw2"), [128, NCH, 128], BF16) as aw2:
            b_c, b_cb, b_ab, b_sig = P.buf("cTs"), P.buf("cTb"), P.buf("adab"), P.buf("sig")
            awr = Rot([(aw0, P.buf("aw0")), (aw1, P.buf("aw1")), (aw2, P.buf("aw2"))])
            P.dma("sp", cTs[:], self.cT[:, :, :], wr=[b_c])
            P.dma("sp", self.CST[:], self.cst[:, :], wr=[self.CONSTb])
            P.op("act", lambda e: e.activation(out=self.ident_b[:], in_=self.CST[:, C_ID:C_ID + 128], func=AF.Copy),
                 rd=[self.CONSTb], wr=[self.CONSTb])
            for l in self.layers:
                P.dma("sp", adab[:, l, :], self.ada_b[self.li[l], :, :], wr=[b_ab])
                P.dma("sp", self.NG[:, l, :], self.norm_g[self.li[l], :, :], wr=[self.CONSTb])
            P.op("dve", lambda e: e.memset(self.ones_bf[:], 1.0 / D), wr=[self.CONSTb])
            P.op("act", lambda e: e.activation(out=sig[:], in_=cTs[:], func=AF.Sigmoid), rd=[b_c], wr=[b_sig])
            P.op("dve", lambda e: e.tensor_tensor(out=cTb[:], in0=cTs[:], in1=sig[:], op=ALU.mult),
                 rd=[b_c, b_sig], wr=[b_cb])
            pi = 0
            for l in self.layers:
                for fc in range(72):
                    wt, wb = awr.next()
                    P.dma("pool", wt[:], self.ada_w[self.li[l], fc, :, :, :], wr=[wb])
                    ps, psb = self.PS[pi % 2], self.PSb[pi % 2]
                    pi += 1
                    P.mm([(lambda e, kc=kc: e.matmul(ps[:, 0:5], wt[:, kc, :], cTb[:, kc, :],
                                                     start=(kc == 0), stop=(kc == NCH - 1)))
                          for kc in range(NCH)], rd=[wb, b_cb], wr=[psb])
                    P.op("dve", lambda e: e.tensor_scalar(out=self.MOD[:, l, fc, :], in0=ps[:, 0:5],
                                                          scalar1=adab[:, l, fc:fc + 1], scalar2=None,
                                                          op0=ALU.add),
                         rd=[psb, b_ab], wr=[self.MODb])
            for l in self.layers:
                for sub in range(3):
                    base = sub * 3 * NCH
                    shift = self.MOD[:, l, base:base + NCH, :]
                    scale = self.MOD[:, l, base + NCH:base + 2 * NCH, :]
                    gate = self.MOD[:, l, base + 2 * NCH:base + 3 * NCH, :]
                    for s in range(5):
                        P.op("dve", lambda e: e.scalar_tensor_tensor(
                            out=self.A1[:, l, sub, :, s], in0=scale[:, :, s], scalar=1.0,
                            in1=self.NG[:, l, sub * NCH:(sub + 1) * NCH], op0=ALU.add, op1=ALU.mult),
                            rd=[self.MODb, self.CONSTb], wr=[self.CONSTb])
                    P.op("dve", lambda e: e.tensor_scalar(out=self.GT[:, l, sub, :, :], in0=gate,
                                                          scalar1=(1.0 if sub == 1 else 0.5), scalar2=None,
                                                          op0=ALU.mult),
                         rd=[self.MODb], wr=[self.CONSTb])
            P.barrier()

    def shift_col(self, l, sub, c, s):
        return self.MOD[:, l, sub * 3 * NCH + c, s:s + 1]

    def load_x(self, s):
        P = self.P
        for ti, (t0, n, isc) in enumerate(TILES):
            P.dma("sp", self.XT[:, :, t0:t0 + n], self.xT[s, :, :, t0:t0 + n], wr=[self.XTb[ti]])

    def store_x(self, s):
        P = self.P
        for ti, (t0, n, isc) in enumerate(TILES):
            if isc:
                continue
            P.dma("sp", self.outT[s, :, :, t0 - CTX:t0 - CTX + n], self.XT[:, :, t0:t0 + n],
                  rd=[self.XTb[ti]], skey="d:sp:out")

    def mod_norm(self, l, sub, s, scr):
        P = self.P
        sq, sqb, rs, rsb, tmp, tmpb = scr
        for ti, (t0, n, isc) in enumerate(TILES):
            ss = 4 if isc else s
            ps, psb = self.PS[ti % 2], self.PSb[ti % 2]
            for c in range(NCH):
                P.op("pool", lambda e, c=c: e.tensor_tensor(out=sq[:, c, :n], in0=self.XT[:, c, t0:t0 + n],
                                                           in1=self.XT[:, c, t0:t0 + n], op=ALU.mult),
                     rd=[self.XTb[ti]], wr=[sqb])
            P.mm([(lambda e, c=c: e.matmul(ps[:, :n], self.ones_bf[:], sq[:, c, :n],
                                           start=(c == 0), stop=(c == NCH - 1))) for c in range(NCH)],
                 rd=[sqb, self.CONSTb], wr=[psb])
            P.op("act", lambda e: e.activation(out=rs[:, :n], in_=ps[:, :n], func=AF.Ln, bias=EPS, scale=1.0),
                 rd=[psb], wr=[rsb])
            P.op("act", lambda e: e.activation(out=rs[:, :n], in_=rs[:, :n], func=AF.Exp, scale=-0.5),
                 rd=[rsb], wr=[rsb])
            for c in range(NCH):
                tb = tmpb[c % 2]
                tt = tmp[c % 2]
                P.op("dve", lambda e, c=c, tt=tt: e.tensor_tensor(out=tt[:, :n], in0=self.XT[:, c, t0:t0 + n],
                                                                 in1=rs[:, :n], op=ALU.mult),
                     rd=[self.XTb[ti], rsb], wr=[tb])
                P.op("act", lambda e, c=c, tt=tt: e.activation(
                    out=self.HT[:, c, t0:t0 + n], in_=tt[:, :n], func=AF.Identity,
                    scale=self.A1[:, l, sub, c, ss:ss + 1], bias=self.shift_col(l, sub, c, ss)),
                    rd=[tb, self.CONSTb, self.MODb], wr=[self.HTb[ti]])

    def ffn(self, l, i, s):
        nc, P = self.nc, self.P
        sub = 0 if i == 0 else 2
        with nc.sbuf_tensor(self.U("sq"), [128, NCH, 512], BF16) as sq, \
                nc.sbuf_tensor(self.U("rs"), [128, 512], F32) as rs, \
                nc.sbuf_tensor(self.U("tmp0"), [128, 512], F32) as tmp0, \
                nc.sbuf_tensor(self.U("tmp1"), [128, 512], F32) as tmp1, \
                nc.sbuf_tensor(self.U("G0"), [128, 4, T], BF16) as G0, \
                nc.sbuf_tensor(self.U("G1"), [128, 4, T], BF16) as G1, \
                nc.sbuf_tensor(self.U("w1s"), [128, 6, NCH, 128], BF16) as w1s, \
                nc.sbuf_tensor(self.U("w2s"), [128, 3, 4, 128], BF16) as w2s, \
                nc.sbuf_tensor(self.U("sa"), [128, 2, 512], F32) as sa:
            scr = (sq, P.buf("sq"), rs, P.buf("rs"), (tmp0, tmp1), (P.buf("tmp0"), P.buf("tmp1")))
            self.mod_norm(l, sub, s, scr)
            Gs = [G0, G1]
            Gb = [[P.buf(f"G{g}_{ti}") for ti in range(len(TILES))] for g in range(2)]
            w1r = Rot([(w1s[:, k], P.buf(f"w1s{k}")) for k in range(6)])
            w2r = Rot([(w2s[:, k], P.buf(f"w2s{k}")) for k in range(3)])
            sab = [P.buf("sa0"), P.buf("sa1")]
            pi = 0
            for fg, js in enumerate(FGROUPS):
                G = Gs[fg % 2]
                gb = Gb[fg % 2]
                for jj, j in enumerate(js):
                    wa, wab = w1r.next()
                    wb, wbb = w1r.next()
                    P.dma("pool", wa, self.w1[self.li[l], i, j, :, :, :], wr=[wab])
                    P.dma("pool", wb, self.w1[self.li[l], i, NFB + j, :, :, :], wr=[wbb])
                    for ti, (t0, n, isc) in enumerate(TILES):
                        pa, pab = self.PS[(pi % 2) * 2], self.PSb[(pi % 2) * 2]
                        pb, pbb = self.PS[(pi % 2) * 2 + 1], self.PSb[(pi % 2) * 2 + 1]
                        sat, satb = sa[:, pi % 2], sab[pi % 2]
                        pi += 1
                        P.mm([(lambda e, kc=kc: e.matmul(pa[:, :n], wa[:, kc, :], self.HT[:, kc, t0:t0 + n],
                                                         start=(kc == 0), stop=(kc == NCH - 1)))
                              for kc in range(NCH)], rd=[wab, self.HTb[ti]], wr=[pab])
                        P.mm([(lambda e, kc=kc: e.matmul(pb[:, :n], wb[:, kc, :], self.HT[:, kc, t0:t0 + n],
                                                         start=(kc == 0), stop=(kc == NCH - 1)))
                              for kc in range(NCH)], rd=[wbb, self.HTb[ti]], wr=[pbb])
                        P.op("act", lambda e: e.activation(out=sat[:, :n], in_=pa[:, :n], func=AF.Silu),
                             rd=[pab], wr=[satb])
                        P.op("dve", lambda e: e.tensor_tensor(out=G[:, jj, t0:t0 + n], in0=pb[:, :n],
                                                              in1=sat[:, :n], op=ALU.mult),
                             rd=[pbb, satb], wr=[gb[ti]])
                for c in range(NCH):
                    w2t, w2b = w2r.next()
                    P.dma("pool", w2t[:, 0:len(js), :], self.w2[self.li[l], i, c, :, js[0]:js[0] + len(js), :], wr=[w2b])
                    for ti, (t0, n, isc) in enumerate(TILES):
                        ss = 4 if isc else s
                        po, pob = self.PS[4 + (pi % 2)], self.PSb[4 + (pi % 2)]
                        pi += 1
                        P.mm([(lambda e, jj=jj: e.matmul(po[:, :n], w2t[:, jj, :], G[:, jj, t0:t0 + n],
                                                         start=(jj == 0), stop=(jj == len(js) - 1)))
                              for jj in range(len(js))], rd=[w2b, gb[ti]], wr=[pob])
                        P.op("dve", lambda e: e.scalar_tensor_tensor(
                            out=self.XT[:, c, t0:t0 + n], in0=po[:, :n], scalar=self.GT[:, l, sub, c, ss:ss + 1],
                            in1=self.XT[:, c, t0:t0 + n], op0=ALU.mult, op1=ALU.add),
                            rd=[pob, self.CONSTb, self.XTb[ti]], wr=[self.XTb[ti]])
            P.barrier()


    def _norm_only(self, l, sub, s):
        nc, P = self.nc, self.P
        with nc.sbuf_tensor(self.U("sq"), [128, NCH, 512], BF16) as sq, \
                nc.sbuf_tensor(self.U("rs"), [128, 512], F32) as rs, \
                nc.sbuf_tensor(self.U("tmp0"), [128, 512], F32) as tmp0, \
                nc.sbuf_tensor(self.U("tmp1"), [128, 512], F32) as tmp1:
            scr = (sq, P.buf("sq"), rs, P.buf("rs"), (tmp0, tmp1), (P.buf("tmp0"), P.buf("tmp1")))
            self.mod_norm(l, sub, s, scr)
            P.barrier()

    def res_add(self, l, s, sti, xt, b_xt, n_in, lhs_fn, rhs_fn, rd):
        P, PS, PSb = self.P, self.PS, self.PSb
        t0, n, isc = TILES[sti]
        ss = 4 if isc else s
        P.dma("sp", xt[:, :, :n], self.xs[:, :, t0:t0 + n], rd=[self.XSb[sti]], wr=[b_xt])
        for c in range(NCH):
            pb = c % 2
            P.mm([(lambda e, fc=fc, c=c, pb=pb: e.matmul(PS[pb][:, :n], lhs_fn(fc, c), rhs_fn(fc),
                                                         start=(fc == 0), stop=(fc == n_in - 1)))
                  for fc in range(n_in)], rd=rd, wr=[PSb[pb]])
            P.op("dve", lambda e, c=c, pb=pb: e.scalar_tensor_tensor(
                out=xt[:, c, :n], in0=PS[pb][:, :n], scalar=self.GT[:, l, 1, c, ss:ss + 1], in1=xt[:, c, :n],
                op0=ALU.mult, op1=ALU.add), rd=[PSb[pb], self.CONSTb, b_xt], wr=[b_xt])
        P.dma("sp", self.xs[:, :, t0:t0 + n], xt[:, :, :n], rd=[b_xt], wr=[self.XSb[sti]])

    def mix_ret(self, l, s):
        nc, P = self.nc, self.P
        sub = 1
        PS, PSb = self.PS, self.PSb
        with ExitStack() as es:
            xt = es.enter_context(nc.sbuf_tensor(self.U("r_xt"), [128, NCH, 512], F32))
            wqk = es.enter_context(nc.sbuf_tensor(self.U("r_wqk"), [128, 4, NCH, 128], BF16))
            wvg = es.enter_context(nc.sbuf_tensor(self.U("r_wvg"), [128, 3, NCH, 512], BF16))
            wo = es.enter_context(nc.sbuf_tensor(self.U("r_wo"), [128, 4, D], BF16))
            gain = es.enter_context(nc.sbuf_tensor(self.U("r_gain"), [128, 2, 512], F32))
            rc = es.enter_context(nc.sbuf_tensor(self.U("r_c"), [128, RET_CW], F32))
            cs = es.enter_context(nc.sbuf_tensor(self.U("r_cs"), [128, 2, 512], F32))
            qT = es.enter_context(nc.sbuf_tensor(self.U("r_qT"), [128, 2, 512], BF16))
            kT = es.enter_context(nc.sbuf_tensor(self.U("r_kT"), [128, 2, 512], BF16))
            rt = es.enter_context(nc.sbuf_tensor(self.U("r_rt"), [128, 4, 512], F32))
            qin = es.enter_context(nc.sbuf_tensor(self.U("r_qin"), [128, 2, 2, 128], BF16))
            V = es.enter_context(nc.sbuf_tensor(self.U("r_V"), [128, 2, 512], BF16))
            Kst = es.enter_context(nc.sbuf_tensor(self.U("r_Kst"), [128, 2, 256], BF16))
            ST = es.enter_context(nc.sbuf_tensor(self.U("r_ST"), [128, 2, 128], BF16))
            S32 = es.enter_context(nc.sbuf_tensor(self.U("r_S32"), [128, 2, 512], F32))
            Sb = es.enter_context(nc.sbuf_tensor(self.U("r_Sb"), [128, 2, 2, 512], BF16))
            Yf = es.enter_context(nc.sbuf_tensor(self.U("r_Yf"), [128, 18, 512], BF16))
            sg = es.enter_context(nc.sbuf_tensor(self.U("r_sg"), [128, 2, 512], F32))
            yn = es.enter_context(nc.sbuf_tensor(self.U("r_yn"), [128, 2, 512], F32))
            yb = es.enter_context(nc.sbuf_tensor(self.U("r_y"), [128, 512], BF16))
            yT = es.enter_context(nc.sbuf_tensor(self.U("r_yT"), [128, 4, 512], BF16))
            st = es.enter_context(nc.sbuf_tensor(self.U("r_st"), [128, 2, 8], F32))
            b_w = P.buf("r_w")
            b_xt = P.buf("r_xt")
            b_cs, b_qT, b_kT, b_rt = P.buf("r_cs"), P.buf("r_qT"), P.buf("r_kT"), P.buf("r_rt")
            b_qin = [P.buf("r_qin0"), P.buf("r_qin1")]
            b_V = [P.buf("r_V0"), P.buf("r_V1")]
            b_K = [P.buf("r_K0"), P.buf("r_K1")]
            b_ST = [P.buf("r_ST0"), P.buf("r_ST1")]
            b_S32 = P.buf("r_S32")
            b_Sb = [P.buf("r_Sb0"), P.buf("r_Sb1")]
            b_Yf = [P.buf(f"r_Yf{i}") for i in range(18)]
            b_sg = [P.buf("r_sg0"), P.buf("r_sg1")]
            b_yn = [P.buf("r_yn0"), P.buf("r_yn1")]
            b_y, b_yT = P.buf("r_y"), P.buf("r_yT")
            b_st = [P.buf("r_st0"), P.buf("r_st1")]
            ident = self.ident_b
            it = 0
            for h in range(4):
                for blk in range(4):
                    P.dma("pool", wqk[:, blk], self.ret_wqk[h, blk, :, :, :], wr=[b_w])
                for k in range(3):
                    P.dma("pool", wvg[:, k], self.ret_wvg[h, k, :, :, :], wr=[b_w])
                P.dma("pool", wo[:], self.ret_wo[h, :, :, :], wr=[b_w])
                P.dma("sp", gain[:], self.ret_gain[:, :, h * 512:(h + 1) * 512], wr=[b_w])
                P.dma("sp", rc[:], self.ret_c[h, :, :], wr=[b_w])
                cd = _ret_gamma128(h)
                for dirn in range(2):
                    P.op("dve", lambda e: e.memset(S32[:], 0.0), wr=[b_S32])
                    P.op("pool", lambda e: e.memset(Sb[:, 0], 0.0), wr=[b_Sb[0]])
                    P.op("pool", lambda e: e.memset(Sb[:, 1], 0.0), wr=[b_Sb[1]])
                    sbi = 0
                    order = list(range(len(TILES)))
                    if dirn == 1:
                        order = [0] + order[:0:-1]
                    for sti in order:
                        t0, n, isc = TILES[sti]
                        ss = 4 if isc else s
                        if not isc:
                            P.dma("sp", cs[:, 0, :n], self.ret_cs[0, :, t0 - CTX:t0 - CTX + n], wr=[b_cs])
                            P.dma("sp", cs[:, 1, :n], self.ret_cs[1, :, t0 - CTX:t0 - CTX + n], wr=[b_cs])
                        for qk in range(2):
                            dst, dstb = (qT, b_qT) if qk == 0 else (kT, b_kT)
                            for hc in range(2):
                                blk = qk * 2 + hc
                                P.mm([(lambda e, kc=kc, blk=blk, hc=hc: e.matmul(
                                    PS[hc][:, :n], wqk[:, blk, kc, :], self.HT[:, kc, t0:t0 + n],
                                    start=(kc == 0), stop=(kc == NCH - 1))) for kc in range(NCH)],
                                    rd=[b_w, self.HTb[sti]], wr=[PSb[hc]])
                            if isc:
                                for hc in range(2):
                                    P.op("act", lambda e, hc=hc: e.activation(out=dst[:, hc, :n], in_=PS[hc][:, :n],
                                                                             func=AF.Copy),
                                         rd=[PSb[hc]], wr=[dstb])
                            else:
                                x1, x2 = PS[0], PS[1]
                                P.op("dve", lambda e: e.tensor_tensor(out=rt[:, 0, :n], in0=x1[:, :n], in1=cs[:, 0, :n],
                                                                      op=ALU.mult), rd=[PSb[0], b_cs], wr=[b_rt])
                                P.op("dve", lambda e: e.tensor_tensor(out=rt[:, 1, :n], in0=x2[:, :n], in1=cs[:, 1, :n],
                                                                      op=ALU.mult), rd=[PSb[1], b_cs], wr=[b_rt])
                                P.op("dve", lambda e: e.tensor_tensor(out=rt[:, 2, :n], in0=x1[:, :n], in1=cs[:, 1, :n],
                                                                      op=ALU.mult), rd=[PSb[0], b_cs], wr=[b_rt])
                                P.op("dve", lambda e: e.tensor_tensor(out=rt[:, 3, :n], in0=x2[:, :n], in1=cs[:, 0, :n],
                                                                      op=ALU.mult), rd=[PSb[1], b_cs], wr=[b_rt])
                                P.op("pool", lambda e: e.tensor_tensor(out=dst[:, 0, :n], in0=rt[:, 0, :n],
                                                                       in1=rt[:, 1, :n], op=ALU.subtract),
                                     rd=[b_rt], wr=[dstb])
                                P.op("pool", lambda e: e.tensor_tensor(out=dst[:, 1, :n], in0=rt[:, 2, :n],
                                                                       in1=rt[:, 3, :n], op=ALU.add),
                                     rd=[b_rt], wr=[dstb])
                        subt = list(range(n // 128))
                        if dirn == 1:
                            subt = subt[::-1]
                        for k in subt:
                            r = it % 2
                            it += 1
                            ti = (t0 // 128) + k
                            c0 = k * 128
                            tt0 = t0 + c0
                            P.mm([(lambda e, kc=kc: e.matmul(PS[2][:, :], self.HT[:, kc, tt0:tt0 + 128], wvg[:, 0, kc, :],
                                                             start=(kc == 0), stop=(kc == NCH - 1)))
                                  for kc in range(NCH)], rd=[b_w, self.HTb[sti]], wr=[PSb[2]])
                            P.op("act", lambda e: e.activation(out=V[:, r, :], in_=PS[2][:, :], func=AF.Copy),
                                 rd=[PSb[2]], wr=[b_V[r]])
                            P.mm([(lambda e, kc=kc: e.matmul(PS[3][:, :], self.HT[:, kc, tt0:tt0 + 128],
                                                             wvg[:, 1 + dirn, kc, :],
                                                             start=(kc == 0), stop=(kc == NCH - 1)))
                                  for kc in range(NCH)], rd=[b_w, self.HTb[sti]], wr=[PSb[3]])
                            P.op("act", lambda e: e.activation(out=sg[:, r, :], in_=PS[3][:, :], func=AF.Silu),
                                 rd=[PSb[3]], wr=[b_sg[r]])
                            P.mm([(lambda e, hc=hc: e.transpose(self.PT[:, hc * 128:(hc + 1) * 128],
                                                                kT[:, hc, c0:c0 + 128], ident[:]))
                                  for hc in range(2)], rd=[b_kT, self.CONSTb], wr=[self.PTb])
                            P.op("dve", lambda e: e.tensor_scalar(out=Kst[:, r, :], in0=self.PT[:, 0:256],
                                                                  scalar1=rc[:, 512 + dirn:513 + dirn], scalar2=None,
                                                                  op0=ALU.mult),
                                 rd=[self.PTb, b_w], wr=[b_K[r]])
                            P.mm([(lambda e, hc=hc: e.matmul(PS[4][:, 0:128], kT[:, hc, c0:c0 + 128],
                                                             qT[:, hc, c0:c0 + 128], start=(hc == 0), stop=(hc == 1)))
                                  for hc in range(2)], rd=[b_kT, b_qT], wr=[PSb[4]])
                            P.op("dve", lambda e: e.tensor_tensor(out=ST[:, r, :], in0=PS[4][:, 0:128],
                                                                  in1=rc[:, dirn * 128:(dirn + 1) * 128], op=ALU.mult),
                                 rd=[PSb[4], b_w], wr=[b_ST[r]])
                            P.op("pool", lambda e: e.tensor_tensor(
                                out=qin[:, r, :, :], in0=qT[:, :, c0:c0 + 128],
                                in1=rc[:, 256 + dirn * 128:256 + (dirn + 1) * 128].unsqueeze(1).to_broadcast([128, 2, 128]),
                                op=ALU.mult), rd=[b_qT, b_w], wr=[b_qin[r]])
                            P.mm([lambda e: e.matmul(PS[5][:, :], ST[:, r, :], V[:, r, :], start=True, stop=False),
                                  lambda e: e.matmul(PS[5][:, :], qin[:, r, 0, :], Sb[:, sbi, 0, :], start=False,
                                                     stop=False),
                                  lambda e: e.matmul(PS[5][:, :], qin[:, r, 1, :], Sb[:, sbi, 1, :], start=False,
                                                     stop=True)],
                                 rd=[b_ST[r], b_V[r], b_qin[r], b_Sb[sbi]], wr=[PSb[5]])
                            for hc in range(2):
                                pb = 3 if hc == 0 else 4
                                P.mm([lambda e, hc=hc, pb=pb: e.matmul(PS[pb][:, :], Kst[:, r, hc * 128:(hc + 1) * 128],
                                                                       V[:, r, :], start=True, stop=True)],
                                     rd=[b_K[r], b_V[r]], wr=[PSb[pb]])
                                P.op("dve", lambda e, hc=hc, pb=pb: e.scalar_tensor_tensor(
                                    out=S32[:, hc, :], in0=S32[:, hc, :], scalar=cd, in1=PS[pb][:, :],
                                    op0=ALU.mult, op1=ALU.add), rd=[PSb[pb], b_S32], wr=[b_S32])
                            nsb = 1 - sbi
                            P.op("act", lambda e, nsb=nsb: e.activation(out=Sb[:, nsb], in_=S32[:], func=AF.Copy),
                                 rd=[b_S32], wr=[b_Sb[nsb]])
                            sbi = nsb
                            P.op("dve", lambda e: e.bn_stats(out=st[:, r, 0:6], in_=PS[5][:, :]), rd=[PSb[5]],
                                 wr=[b_st[r]])
                            P.op("dve", lambda e: e.bn_aggr(out=st[:, r, 6:8], in_=st[:, r, 0:6]), rd=[b_st[r]],
                                 wr=[b_st[r]])
                            P.op("dve", lambda e: e.tensor_scalar(out=st[:, r, 0:1], in0=st[:, r, 7:8], scalar1=EPS,
                                                                  scalar2=None, op0=ALU.add), rd=[b_st[r]], wr=[b_st[r]])
                            P.op("pool", lambda e: e.tensor_tensor(out=st[:, r, 1:2], in0=st[:, r, 0:1],
                                                                   in1=self.CST[:, C_NH:C_NH + 1], op=ALU.pow),
                                 rd=[b_st[r], self.CONSTb], wr=[b_st[r]])
                            P.op("dve", lambda e: e.tensor_scalar(out=yn[:, r, :], in0=PS[5][:, :],
                                                                  scalar1=st[:, r, 6:7], scalar2=st[:, r, 1:2],
                                                                  op0=ALU.subtract, op1=ALU.mult),
                                 rd=[PSb[5], b_st[r]], wr=[b_yn[r]])
                            P.op("pool", lambda e: e.tensor_tensor(out=yn[:, r, :], in0=yn[:, r, :],
                                                                   in1=gain[:, dirn, :], op=ALU.mult),
                                 rd=[b_yn[r], b_w], wr=[b_yn[r]])
                            if dirn == 0:
                                P.op("pool", lambda e: e.tensor_tensor(out=Yf[:, ti, :], in0=yn[:, r, :],
                                                                       in1=sg[:, r, :], op=ALU.mult),
                                     rd=[b_yn[r], b_sg[r]], wr=[b_Yf[ti]])
                            else:
                                P.op("pool", lambda e: e.tensor_tensor(out=yn[:, r, :], in0=yn[:, r, :],
                                                                       in1=sg[:, r, :], op=ALU.mult),
                                     rd=[b_yn[r], b_sg[r]], wr=[b_yn[r]])
                                P.op("dve", lambda e: e.tensor_tensor(out=yb[:], in0=yn[:, r, :], in1=Yf[:, ti, :],
                                                                      op=ALU.add),
                                     rd=[b_yn[r], b_Yf[ti]], wr=[b_y])
                                P.mm([(lambda e, fc=fc: e.transpose(self.PT[:, 256 + fc * 128:256 + (fc + 1) * 128],
                                                                    yb[:, fc * 128:(fc + 1) * 128], ident[:]))
                                      for fc in range(4)], rd=[b_y, self.CONSTb], wr=[self.PTb])
                                P.op("act", lambda e: e.activation(
                                    out=yT[:, :, c0:c0 + 128],
                                    in_=self.PT[:, 256:768].rearrange("p (f t) -> p f t", f=4), func=AF.Copy),
                                    rd=[self.PTb], wr=[b_yT])
                        if dirn == 1:
                            self.res_add(l, s, sti, xt, b_xt, 4,
                                         lambda fc, c: wo[:, fc, c * 128:(c + 1) * 128],
                                         lambda fc: yT[:, fc, :n], [b_w, b_yT])
            P.barrier()

    def _rms_heads(self, src, nh, g_ap, dst, stt, sq, b_src, b_stt, b_sq, b_dst, scale):
        P = self.P
        P.op("pool", lambda e: e.tensor_tensor(out=sq[:, :nh * 64], in0=src, in1=src, op=ALU.mult),
             rd=[b_src], wr=[b_sq])
        P.op("dve", lambda e: e.tensor_reduce(out=stt[:, 0:nh], in_=sq[:, :nh * 64].rearrange("p (h d) -> p h d", d=64),
                                              axis=AX.X, op=ALU.add), rd=[b_sq], wr=[b_stt])
        P.op("dve", lambda e: e.tensor_scalar(out=stt[:, 16:16 + nh], in0=stt[:, 0:nh], scalar1=1.0 / 64.0,
                                              scalar2=EPS, op0=ALU.mult, op1=ALU.add), rd=[b_stt], wr=[b_stt])
        P.op("pool", lambda e: e.tensor_tensor(out=stt[:, 32:32 + nh], in0=stt[:, 16:16 + nh],
                                               in1=self.CST[:, C_NH:C_NH + 1].to_broadcast([128, nh]), op=ALU.pow),
             rd=[b_stt, self.CONSTb], wr=[b_stt])
        P.op("dve", lambda e: e.tensor_tensor(
            out=dst, in0=src.rearrange("p (h d) -> p h d", d=64),
            in1=stt[:, 32:32 + nh].unsqueeze(2).to_broadcast([128, nh, 64]), op=ALU.mult),
            rd=[b_src, b_stt], wr=[b_dst])
        P.op("pool", lambda e: e.scalar_tensor_tensor(
            out=dst, in0=dst, scalar=scale, in1=g_ap.unsqueeze(1).to_broadcast([128, nh, 64]),
            op0=ALU.mult, op1=ALU.mult) if False else e.tensor_tensor(
            out=dst, in0=dst, in1=g_ap.unsqueeze(1).to_broadcast([128, nh, 64]), op=ALU.mult),
            rd=[b_dst, self.CONSTb], wr=[b_dst])

    def _rope_heads(self, xn, nh, cs, rt, out1, out2, b_xn, b_cs, b_rt, b_out):
        P = self.P
        x1, x2 = xn[:, :, 0:32], xn[:, :, 32:64]
        cb = cs[:, 0, :].unsqueeze(1).to_broadcast([128, nh, 32])
        sb = cs[:, 1, :].unsqueeze(1).to_broadcast([128, nh, 32])
        P.op("dve", lambda e: e.tensor_tensor(out=rt[:, 0, :nh, :], in0=x1, in1=cb, op=ALU.mult),
             rd=[b_xn, b_cs], wr=[b_rt])
        P.op("pool", lambda e: e.tensor_tensor(out=rt[:, 1, :nh, :], in0=x2, in1=sb, op=ALU.mult),
             rd=[b_xn, b_cs], wr=[b_rt])
        P.op("dve", lambda e: e.tensor_tensor(out=rt[:, 2, :nh, :], in0=x1, in1=sb, op=ALU.mult),
             rd=[b_xn, b_cs], wr=[b_rt])
        P.op("pool", lambda e: e.tensor_tensor(out=rt[:, 3, :nh, :], in0=x2, in1=cb, op=ALU.mult),
             rd=[b_xn, b_cs], wr=[b_rt])
        P.op("dve", lambda e: e.tensor_tensor(out=out1, in0=rt[:, 0, :nh, :], in1=rt[:, 1, :nh, :], op=ALU.subtract),
             rd=[b_rt], wr=[b_out])
        P.op("pool", lambda e: e.tensor_tensor(out=out2, in0=rt[:, 2, :nh, :], in1=rt[:, 3, :nh, :], op=ALU.add),
             rd=[b_rt], wr=[b_out])

    def mix_swa(self, l, s):
        nc, P = self.nc, self.P
        PS, PSb = self.PS, self.PSb
        ident = self.ident_b
        with ExitStack() as es:
            def sb(name, shape, dt):
                return es.enter_context(nc.sbuf_tensor(self.U(name), shape, dt))
            xt = sb("a_xt", [128, NCH, 512], F32)
            wqkv = sb("a_wqkv", [128, NCH, 1536], BF16)
            wo = sb("a_wo", [128, NCH, D], BF16)
            g2 = sb("a_g2", [128, 2, 64], F32)
            esink = sb("a_esink", [128, 16], F32)
            cs = sb("a_cs", [128, 2, 32], F32)
            kv = sb("a_kv", [128, 512], F32)
            sq = sb("a_sq", [128, 1024], F32)
            stt = sb("a_stt", [128, 48], F32)
            xn = sb("a_xn", [128, 16, 64], F32)
            rt = sb("a_rt", [128, 4, 16, 32], F32)
            kdup = sb("a_kdup", [128, 4, 2, 64], BF16)
            kT2 = sb("a_kT2", [128, 4, T], BF16)
            Vaug = sb("a_Vaug", [128, 18, 4, 65], BF16)
            qs = sb("a_qs", [128, 1024], F32)
            qr = sb("a_qr", [128, 16, 64], BF16)
            qT = sb("a_qT", [128, NCH, 2, 128], BF16)
            E = sb("a_E", [128, 2, 4, 128], F32)
            PL = sb("a_PL", [128, 2, 4, 128], BF16)
            yb = sb("a_y", [128, 16, 64], BF16)
            yT = sb("a_yT", [128, NCH, 512], BF16)
            dn = sb("a_dn", [128, 8], F32)
            b_w, b_xt, b_cs, b_kv, b_sq, b_stt, b_xn, b_rt = (P.buf("a_w"), P.buf("a_xt"), P.buf("a_cs"), P.buf("a_kv"),
                                                              P.buf("a_sq"), P.buf("a_stt"), P.buf("a_xn"), P.buf("a_rt"))
            b_kdup, b_qs, b_qr, b_qT, b_y, b_yT, b_dn = (P.buf("a_kdup"), P.buf("a_qs"), P.buf("a_qr"), P.buf("a_qT"),
                                                         P.buf("a_y"), P.buf("a_yT"), P.buf("a_dn"))
            b_kT2 = [P.buf(f"a_kT2_{i}") for i in range(18)]
            b_V = [P.buf(f"a_V{i}") for i in range(18)]
            b_E = [P.buf("a_E0"), P.buf("a_E1")]
            b_PL = [P.buf("a_PL0"), P.buf("a_PL1")]
            for k in range(3):
                P.dma("pool", wqkv[:, :, k * 512:(k + 1) * 512], self.swa_wqkv[:, :, k * 512:(k + 1) * 512], wr=[b_w])
            P.dma("pool", wo[:], self.swa_wo[:, :, :], wr=[b_w])
            P.dma("sp", g2[:], self.swa_g[:, :, :], wr=[b_w])
            P.dma("sp", esink[:], self.swa_sink[:, :], wr=[b_w])
            P.op("act", lambda e: e.activation(out=esink[:], in_=esink[:], func=AF.Exp), rd=[b_w], wr=[b_w])
            P.op("dve", lambda e: e.tensor_scalar(out=g2[:, 0, :], in0=g2[:, 0, :], scalar1=0.125, scalar2=None,
                                                  op0=ALU.mult), rd=[b_w], wr=[b_w])
            P.op("pool", lambda e: e.memset(Vaug[:], 1.0), wr=b_V)
            P.op("pool", lambda e: e.memset(qT[:], 0.0), wr=[b_qT])
            for ti in range(18):
                t0 = ti * 128
                sti = 0 if ti < 2 else 1 + (ti - 2) // 4
                lat = ti >= 2
                P.mm([(lambda e, kc=kc: e.matmul(PS[4][:, :], self.HT[:, kc, t0:t0 + 128], wqkv[:, kc, 1024:1536],
                                                 start=(kc == 0), stop=(kc == NCH - 1))) for kc in range(NCH)],
                     rd=[b_w, self.HTb[sti]], wr=[PSb[4]])
                P.op("act", lambda e: e.activation(out=kv[:], in_=PS[4][:, :], func=AF.Copy), rd=[PSb[4]], wr=[b_kv])
                P.op("act", lambda e: e.activation(out=Vaug[:, ti, :, 0:64],
                                                   in_=kv[:, 256:512].rearrange("p (g d) -> p g d", d=64),
                                                   func=AF.Copy), rd=[b_kv], wr=[b_V[ti]])
                self._rms_heads(kv[:, 0:256], 4, g2[:, 1, :], xn[:, 0:4, :], stt, sq, b_kv, b_stt, b_sq, b_xn, 1.0)
                if lat:
                    P.dma("sp", cs[:], self.swa_cs[t0 - CTX:t0 - CTX + 128, :, :], wr=[b_cs])
                    self._rope_heads(xn[:, 0:4, :], 4, cs, rt, kdup[:, :, 0, 0:32], kdup[:, :, 0, 32:64],
                                     b_xn, b_cs, b_rt, b_kdup)
                else:
                    P.op("act", lambda e: e.activation(out=kdup[:, :, 0, :], in_=xn[:, 0:4, :], func=AF.Copy),
                         rd=[b_xn], wr=[b_kdup])
                P.op("act", lambda e: e.activation(out=kdup[:, :, 1, :], in_=kdup[:, :, 0, :], func=AF.Copy),
                     rd=[b_kdup], wr=[b_kdup])
                P.mm([(lambda e, g=g: e.transpose(self.PT[:, g * 128:(g + 1) * 128],
                                                  kdup[:, g, :, :].rearrange("p a d -> p (a d)"), ident[:]))
                      for g in range(4)], rd=[b_kdup, self.CONSTb], wr=[self.PTb])
                P.op("act", lambda e: e.activation(out=kT2[:, :, t0:t0 + 128],
                                                   in_=self.PT[:, 0:512].rearrange("p (g t) -> p g t", g=4),
                                                   func=AF.Copy), rd=[self.PTb], wr=[b_kT2[ti]])
            ei = 0
            for qi in range(18):
                t0 = qi * 128
                sti = 0 if qi < 2 else 1 + (qi - 2) // 4
                lat = qi >= 2
                c0 = t0 - TILES[sti][0]
                for hf in range(2):
                    P.mm([(lambda e, kc=kc: e.matmul(PS[4 + hf][:, :], self.HT[:, kc, t0:t0 + 128],
                                                     wqkv[:, kc, hf * 512:(hf + 1) * 512],
                                                     start=(kc == 0), stop=(kc == NCH - 1))) for kc in range(NCH)],
                         rd=[b_w, self.HTb[sti]], wr=[PSb[4 + hf]])
                    P.op("act", lambda e: e.activation(out=qs[:, hf * 512:(hf + 1) * 512], in_=PS[4 + hf][:, :],
                                                       func=AF.Copy), rd=[PSb[4 + hf]], wr=[b_qs])
                self._rms_heads(qs[:, :], 16, g2[:, 0, :], xn[:, :, :], stt, sq, b_qs, b_stt, b_sq, b_xn, 1.0)
                if lat:
                    P.dma("sp", cs[:], self.swa_cs[t0 - CTX:t0 - CTX + 128, :, :], wr=[b_cs])
                    self._rope_heads(xn[:, :, :], 16, cs, rt, qr[:, :, 0:32], qr[:, :, 32:64], b_xn, b_cs, b_rt, b_qr)
                else:
                    P.op("act", lambda e: e.activation(out=qr[:], in_=xn[:], func=AF.Copy), rd=[b_xn], wr=[b_qr])
                P.mm([(lambda e, c=c: e.transpose(self.PT[:, c * 128:(c + 1) * 128],
                                                  qr[:, 2 * c:2 * c + 2, :].rearrange("p a d -> p (a d)"), ident[:]))
                      for c in range(NCH)], rd=[b_qr, self.CONSTb], wr=[self.PTb])
                for half in range(2):
                    P.op("act", lambda e, half=half: e.activation(
                        out=qT[64 * half:64 * half + 64, :, half, :],
                        in_=self.PT[64 * half:64 * half + 64, :].rearrange("p (c t) -> p c t", c=NCH),
                        func=AF.Copy), rd=[self.PTb], wr=[b_qT])
                kbs = [(0, None), (1, None)]
                if lat:
                    if qi > 2:
                        kbs.append((qi - 1, C_GE))
                    kbs.append((qi, None))
                    if qi < 17:
                        kbs.append((qi + 1, C_LE))
                for g in range(4):
                    po, pob = PS[2 + g % 2], PSb[2 + g % 2]
                    for kbi, (kt, mcol) in enumerate(kbs):
                        r = ei % 2
                        ei += 1
                        pss, pssb = PS[r], PSb[r]
                        fns = []
                        for j in range(4):
                            hq = 4 * g + j
                            c, half = hq // 2, hq % 2
                            fns.append(lambda e, j=j, c=c, half=half: e.matmul(
                                pss[:, j * 128:(j + 1) * 128], kT2[:, g, kt * 128:(kt + 1) * 128],
                                qT[:, c, half, :], start=True, stop=True))
                        P.mm(fns, rd=[b_kT2[kt], b_qT], wr=[pssb])
                        if mcol is None:
                            P.op("act", lambda e: e.activation(
                                out=PL[:, r], in_=pss[:, :].rearrange("p (j t) -> p j t", j=4), func=AF.Exp),
                                rd=[pssb], wr=[b_PL[r]])
                        else:
                            P.op("act", lambda e: e.activation(
                                out=E[:, r], in_=pss[:, :].rearrange("p (j t) -> p j t", j=4), func=AF.Exp),
                                rd=[pssb], wr=[b_E[r]])
                            P.op("dve", lambda e: e.tensor_tensor(
                                out=PL[:, r], in0=E[:, r],
                                in1=self.CST[:, mcol:mcol + 128].unsqueeze(1).to_broadcast([128, 4, 128]), op=ALU.mult),
                                rd=[b_E[r], self.CONSTb], wr=[b_PL[r]])
                        fns = []
                        for j in range(4):
                            first = (kbi == 0 and j == 0)
                            last = (kbi == len(kbs) - 1 and j == 3)
                            fns.append(lambda e, j=j, first=first, last=last: e.matmul(
                                po[:, j * 65:(j + 1) * 65], PL[:, r, j, :], Vaug[:, kt, g, :],
                                start=first, stop=last, skip_group_check=True))
                        P.mm(fns, rd=[b_PL[r], b_V[kt]], wr=[pob])
                    pv = po[:, 0:260].rearrange("p (j d) -> p j d", d=65)
                    P.op("dve", lambda e: e.tensor_tensor(out=dn[:, 0:4], in0=pv[:, :, 64],
                                                          in1=esink[:, 4 * g:4 * g + 4], op=ALU.add),
                         rd=[pob, b_w], wr=[b_dn])
                    P.op("dve", lambda e: e.reciprocal(out=dn[:, 4:8], in_=dn[:, 0:4]), rd=[b_dn], wr=[b_dn])
                    P.op("dve", lambda e: e.tensor_tensor(
                        out=yb[:, 4 * g:4 * g + 4, :], in0=pv[:, :, 0:64],
                        in1=dn[:, 4:8].unsqueeze(2).to_broadcast([128, 4, 64]), op=ALU.mult),
                        rd=[pob, b_dn], wr=[b_y])
                P.mm([(lambda e, c=c: e.transpose(self.PT[:, c * 128:(c + 1) * 128],
                                                  yb[:, 2 * c:2 * c + 2, :].rearrange("p a d -> p (a d)"), ident[:]))
                      for c in range(NCH)], rd=[b_y, self.CONSTb], wr=[self.PTb])
                P.op("act", lambda e: e.activation(out=yT[:, :, c0:c0 + 128],
                                                   in_=self.PT[:, :].rearrange("p (c t) -> p c t", c=NCH),
                                                   func=AF.Copy), rd=[self.PTb], wr=[b_yT])
                if c0 + 128 == TILES[sti][1]:
                    n = TILES[sti][1]
                    self.res_add(l, s, sti, xt, b_xt, NCH, lambda fc, c: wo[:, fc, c * 128:(c + 1) * 128],
                                 lambda fc: yT[:, fc, :n], [b_w, b_yT])
            P.barrier()

    def mix_dn(self, l, s):
        nc, P = self.nc, self.P
        PS = self.PS
        slot_id = l // 3
        ident = self.ident_b
        CST = self.CST
        BIG = 30000.0
        with ExitStack() as es:
            def sb(name, shape, dt):
                return es.enter_context(nc.sbuf_tensor(self.U(name), shape, dt))
            xt = sb("d_xt", [128, NCH, 512], F32)
            b_xt = P.buf("d_xt")
            b_w = P.buf("d_w")
            cm = sb("d_cm", [128, 2, 7, 128], BF16)
            nm = sb("d_nm", [128, 2, 128], F32)
            ab_c = sb("d_abc", [128, 2, 16], F32)
            ogain = sb("d_og", [128, 128], F32)
            convw = sb("d_cw", [128, 24, 5], F32)
            wab = sb("d_wab", [128, NCH, 32], BF16)
            P.dma("pool", cm[:], self.dn_masks[:, :, :, :], wr=[b_w])
            P.dma("sp", ab_c[:], self.dn_ab[slot_id, :, :, :], wr=[b_w])
            P.dma("sp", ogain[:], self.dn_og[slot_id, :, :], wr=[b_w])
            P.dma("sp", convw[:], self.dn_cw[slot_id, :, :, :], wr=[b_w])
            P.dma("pool", wab[:], self.dn_wab[slot_id, :, :, :], wr=[b_w])
            P.op("dve", lambda e: e.tensor_scalar(out=nm[:, 0, :], in0=CST[:, C_GE:C_GE + 128], scalar1=-1.0,
                                                  scalar2=BIG, op0=ALU.add, op1=ALU.mult), rd=[self.CONSTb], wr=[b_w])
            P.op("dve", lambda e: e.tensor_scalar(out=nm[:, 1, :], in0=CST[:, C_LE:C_LE + 128], scalar1=-1.0,
                                                  scalar2=BIG, op0=ALU.add, op1=ALU.mult), rd=[self.CONSTb], wr=[b_w])
            tri_bf = sb("d_tribf", [128, 2, 128], BF16)
            one_bf = sb("d_onebf", [128, 128], BF16)
            P.op("act", lambda e: e.activation(out=tri_bf[:, 0, :], in_=CST[:, C_LE:C_LE + 128], func=AF.Copy),
                 rd=[self.CONSTb], wr=[b_w])
            P.op("act", lambda e: e.activation(out=tri_bf[:, 1, :], in_=CST[:, C_GE:C_GE + 128], func=AF.Copy),
                 rd=[self.CONSTb], wr=[b_w])
            P.op("act", lambda e: e.activation(out=one_bf[:], in_=CST[:, C_ONE:C_ONE + 128], func=AF.Copy),
                 rd=[self.CONSTb], wr=[b_w])
            P.op("act", lambda e: e.activation(out=ab_c[:, 0, :], in_=ab_c[:, 0, :], func=AF.Exp), rd=[b_w], wr=[b_w])
            P.op("dve", lambda e: e.tensor_scalar(out=ab_c[:, 0, :], in0=ab_c[:, 0, :], scalar1=-1.0, scalar2=None,
                                                  op0=ALU.mult), rd=[b_w], wr=[b_w])
            GBg = sb("d_GBg", [128, 18, 16], F32)
            GBb = sb("d_GBb", [128, 18, 16], F32)
            gt = sb("d_gt", [128, 6, 16], F32)
            b_GB = [P.buf(f"d_GB{i}") for i in range(18)]
            b_gt = P.buf("d_gt")
            psq = [[self.PSb[b]] * 4 for b in range(6)]
            for ti in range(18):
                t0 = ti * 128
                sti = 0 if ti < 2 else 1 + (ti - 2) // 4
                pg, pgb = PS[4][:, 0:32], psq[4][0]
                P.mm([(lambda e, kc=kc: e.matmul(pg, self.HT[:, kc, t0:t0 + 128], wab[:, kc, :],
                                                 start=(kc == 0), stop=(kc == NCH - 1))) for kc in range(NCH)],
                     rd=[b_w, self.HTb[sti]], wr=[pgb])
                pv = pg.rearrange("p (d k h) -> p d k h", d=2, k=2)
                xa = gt[:, 0, :].rearrange("p (d h) -> p d h", d=2)
                P.op("dve", lambda e: e.tensor_tensor(out=xa, in0=pv[:, :, 0, :],
                                                      in1=ab_c[:, 1, :].rearrange("p (d h) -> p d h", d=2), op=ALU.add),
                     rd=[pgb, b_w], wr=[b_gt])
                P.op("dve", lambda e: e.scalar_tensor_tensor(out=gt[:, 1, :], in0=gt[:, 0, :], scalar=-1.0,
                                                             in1=gt[:, 0, :], op0=ALU.mult, op1=ALU.max),
                     rd=[b_gt], wr=[b_gt])
                P.op("act", lambda e: e.activation(out=gt[:, 2, :], in_=gt[:, 1, :], func=AF.Exp, scale=-1.0),
                     rd=[b_gt], wr=[b_gt])
                P.op("act", lambda e: e.activation(out=gt[:, 2, :], in_=gt[:, 2, :], func=AF.Ln, bias=1.0, scale=1.0),
                     rd=[b_gt], wr=[b_gt])
                P.op("dve", lambda e: e.scalar_tensor_tensor(out=gt[:, 3, :], in0=gt[:, 0, :], scalar=0.0,
                                                             in1=gt[:, 2, :], op0=ALU.max, op1=ALU.add),
                     rd=[b_gt], wr=[b_gt])
                P.op("dve", lambda e: e.tensor_tensor(out=GBg[:, ti, :], in0=gt[:, 3, :], in1=ab_c[:, 0, :], op=ALU.mult),
                     rd=[b_gt, b_w], wr=[b_GB[ti]])
                P.op("act", lambda e: e.activation(out=gt[:, 4, :].rearrange("p (d h) -> p d h", d=2),
                                                   in_=pv[:, :, 1, :], func=AF.Exp, scale=-1.0),
                     rd=[pgb], wr=[b_gt])
                P.op("dve", lambda e: e.tensor_scalar(out=gt[:, 5, :], in0=gt[:, 4, :], scalar1=1.0, scalar2=None,
                                                      op0=ALU.add), rd=[b_gt], wr=[b_gt])
                P.op("dve", lambda e: e.reciprocal(out=GBb[:, ti, :], in_=gt[:, 5, :]), rd=[b_gt], wr=[b_GB[ti]])
            import os as _os
            STOP = int(_os.environ.get("DN_STOP", "99"))
            if STOP == 1:
                P.barrier()
                return
            wblk = sb("d_wblk", [128, 2, NCH, 128], BF16)
            wz = sb("d_wz", [128, NCH, 128], BF16)
            woh = sb("d_woh", [128, D], BF16)
            PRE = sb("d_PRE", [128, T + 8], F32)
            ACC = sb("d_ACC", [128, T], F32)
            SQ = sb("d_SQ", [128, 512], BF16)
            rs = sb("d_rs", [128, 512], F32)
            QKV = sb("d_QKV", [128, 3, T], BF16)
            Od = sb("d_O", [128, 2, 18, 128], BF16)
            b_wblk = [P.buf("d_wblk0"), P.buf("d_wblk1")]
            b_wz, b_woh, b_PRE, b_ACC, b_SQ, b_rs = (P.buf("d_wz"), P.buf("d_woh"), P.buf("d_PRE"), P.buf("d_ACC"),
                                                     P.buf("d_SQ"), P.buf("d_rs"))
            b_QKV = [[P.buf(f"d_QKV{j}_{i}") for i in range(len(TILES))] for j in range(3)]
            b_O = [[P.buf(f"d_O{d}_{i}") for i in range(18)] for d in range(2)]
            P.op("pool", lambda e: e.memset(PRE[:], 0.0), wr=[b_PRE])
            TS = []
            for d in range(2):
                t = {}
                for nme, shp, dt in (("gbc", [128, 2, 128], BF16), ("ghl", [128, 4], BF16), ("gl32", [128, 2], F32),
                                     ("t1", [128, 128], F32), ("t2", [128, 128], F32),
                                     ("D", [128, 128], F32), ("DT", [128, 128], F32), ("egc", [128, 128], F32),
                                     ("L", [128, 128], F32), ("Call", [128, 7, 128], BF16),
                                     ("CTall", [128, 7, 128], BF16), ("T32", [128, 2, 128], F32),
                                     ("TT32", [128, 2, 128], F32), ("Tbf", [128, 2, 128], BF16),
                                     ("TTbf", [128, 2, 128], BF16), ("W", [128, 2, 128], BF16), ("sc", [128, 8], F32)):
                    t[nme] = sb(f"d_{nme}{d}", shp, dt)
                    t["b_" + nme] = P.buf(f"d_{nme}{d}")
                t["b_T"] = [P.buf(f"d_T{d}_0"), P.buf(f"d_T{d}_1")]
                t["b_TT"] = [P.buf(f"d_TT{d}_0"), P.buf(f"d_TT{d}_1")]
                t["b_W"] = [P.buf(f"d_W{d}_0"), P.buf(f"d_W{d}_1")]
                TS.append(t)
            SL = [[None, None], [None, None]]
            for d in range(2):
                for k in range(2):
                    sl = {}
                    for nme in ("TT", "kbg", "kd", "vb", "qkT", "qgT"):
                        sl[nme] = sb(f"d_s{nme}{d}{k}", [128, 128], BF16)
                        sl["b_" + nme] = P.buf(f"d_s{nme}{d}{k}")
                    sl["a"] = sb(f"d_sa{d}{k}", [128, 8], F32)
                    sl["b_a"] = P.buf(f"d_sa{d}{k}")
                    SL[d][k] = sl
            RS = []
            for d in range(2):
                r = {}
                r["S32"] = sb(f"d_S32{d}", [128, 128], F32)
                r["Sbf"] = sb(f"d_Sbf{d}", [128, 2, 128], BF16)
                r["nwT"] = sb(f"d_nwT{d}", [128, 128], BF16)
                r["vnew"] = sb(f"d_vnew{d}", [128, 128], BF16)
                r["b_S32"] = P.buf(f"d_S32{d}")
                r["b_Sbf"] = [P.buf(f"d_Sbf{d}0"), P.buf(f"d_Sbf{d}1")]
                r["b_nwT"] = P.buf(f"d_nwT{d}")
                r["b_vnew"] = P.buf(f"d_vnew{d}")
                RS.append(r)
            osum = sb("d_osum", [128, 128], F32)
            osq = sb("d_osq", [128, 128], F32)
            ost = sb("d_ost", [128, 4], F32)
            zs = sb("d_zs", [128, 128], F32)
            ybf = sb("d_y", [128, 128], BF16)
            yT = sb("d_yT", [128, 512], BF16)
            b_osum, b_osq, b_ost, b_zs, b_y, b_yT = (P.buf("d_osum"), P.buf("d_osq"), P.buf("d_ost"), P.buf("d_zs"),
                                                     P.buf("d_y"), P.buf("d_yT"))
            b_PT = [P.buf("d_PT0", excl=True), P.buf("d_PT1", excl=True)]
            b_PT = b_PT + b_PT
            qctr = [0] * 6

            def nq(bank):
                q = qctr[bank] % 4
                qctr[bank] += 1
                return PS[bank][:, q * 128:(q + 1) * 128], psq[bank][q]

            def pcol(t):
                return t + 2 if t < CTX else t + 6

            import os as _os
            for h in range(int(_os.environ.get('DN_HEADS', '8'))):
                for j in range(3):
                    blk = j * 8 + h
                    wb_t, wb_b = wblk[:, j % 2], b_wblk[j % 2]
                    P.dma("pool", wb_t, self.dn_wqkv[slot_id, blk, :, :, :], wr=[wb_b])
                    for sti, (t0, n, isc) in enumerate(TILES):
                        pb = sti % 2
                        P.mm([(lambda e, kc=kc, pb=pb: e.matmul(PS[4 + pb][:, :n], wb_t[:, kc, :],
                                                                self.HT[:, kc, t0:t0 + n],
                                                                start=(kc == 0), stop=(kc == NCH - 1)))
                              for kc in range(NCH)], rd=[wb_b, self.HTb[sti]], wr=[psq[4 + pb][0]])
                        P.op("act", lambda e, pb=pb: e.activation(out=PRE[:, pcol(t0):pcol(t0) + n],
                                                                  in_=PS[4 + pb][:, :n], func=AF.Copy),
                             rd=[psq[4 + pb][0]], wr=[b_PRE])
                    for (base, ln, a0) in ((2, CTX, 0), (CTX + 6, SEQ, CTX)):
                        P.op("dve", lambda e: e.tensor_scalar(out=ACC[:, a0:a0 + ln], in0=PRE[:, base - 2:base - 2 + ln],
                                                              scalar1=convw[:, blk, 0:1], scalar2=None, op0=ALU.mult),
                             rd=[b_PRE, b_w], wr=[b_ACC])
                        for k in range(1, 5):
                            P.op("dve", lambda e, k=k: e.scalar_tensor_tensor(
                                out=ACC[:, a0:a0 + ln], in0=PRE[:, base - 2 + k:base - 2 + k + ln],
                                scalar=convw[:, blk, k:k + 1], in1=ACC[:, a0:a0 + ln], op0=ALU.mult, op1=ALU.add),
                                rd=[b_PRE, b_w, b_ACC], wr=[b_ACC])
                    if j == 2:
                        for sti, (t0, n, isc) in enumerate(TILES):
                            P.op("act", lambda e: e.activation(out=QKV[:, 2, t0:t0 + n], in_=ACC[:, t0:t0 + n],
                                                               func=AF.Silu), rd=[b_ACC], wr=[b_QKV[2][sti]])
                    else:
                        P.op("act", lambda e: e.activation(out=ACC[:, :], in_=ACC[:, :], func=AF.Silu),
                             rd=[b_ACC], wr=[b_ACC])
                        for sti, (t0, n, isc) in enumerate(TILES):
                            P.op("pool", lambda e: e.tensor_tensor(out=SQ[:, :n], in0=ACC[:, t0:t0 + n],
                                                                   in1=ACC[:, t0:t0 + n], op=ALU.mult),
                                 rd=[b_ACC], wr=[b_SQ])
                            pb = sti % 2
                            P.mm([lambda e, pb=pb: e.matmul(PS[4 + pb][:, :n], self.ones_bf[:], SQ[:, :n],
                                                            start=True, stop=True)],
                                 rd=[b_SQ, self.CONSTb], wr=[psq[4 + pb][0]])
                            P.op("act", lambda e, pb=pb: e.activation(out=rs[:, :n], in_=PS[4 + pb][:, :n], func=AF.Ln,
                                                                      bias=EPS, scale=float(D)),
                                 rd=[psq[4 + pb][0]], wr=[b_rs])
                            P.op("act", lambda e: e.activation(out=rs[:, :n], in_=rs[:, :n], func=AF.Exp, scale=-0.5),
                                 rd=[b_rs], wr=[b_rs])
                            qs = (128.0 ** -0.5) if j == 0 else 1.0
                            P.op("dve", lambda e, qs=qs: e.scalar_tensor_tensor(
                                out=QKV[:, j, t0:t0 + n], in0=ACC[:, t0:t0 + n], scalar=qs, in1=rs[:, :n],
                                op0=ALU.mult, op1=ALU.mult), rd=[b_ACC, b_rs], wr=[b_QKV[j][sti]])
                P.dma("pool", wz[:], self.dn_wz[slot_id, h, :, :, :], wr=[b_wz])
                P.dma("pool", woh[:], self.dn_wo[slot_id, h, :, :], wr=[b_woh])

                if STOP == 2:
                    P.barrier()
                    return
                def gen_T(d, ti, sl):
                    ts = TS[d]
                    t0 = ti * 128
                    sti = 0 if ti < 2 else 1 + (ti - 2) // 4
                    col = d * 8 + h
                    gcol, bcol = GBg[:, ti, col:col + 1], GBb[:, ti, col:col + 1]
                    tri = CST[:, C_LE:C_LE + 128] if d == 0 else CST[:, C_GE:C_GE + 128]
                    nml, nmu = (nm[:, 0, :], nm[:, 1, :]) if d == 0 else (nm[:, 1, :], nm[:, 0, :])
                    cma, cmb = (cm[:, 0], cm[:, 1]) if d == 0 else (cm[:, 1], cm[:, 0])
                    bank = d
                    sc = ts["sc"]
                    kT, qTt, vT = QKV[:, 1, t0:t0 + 128], QKV[:, 0, t0:t0 + 128], QKV[:, 2, t0:t0 + 128]
                    bq, bk, bv = b_QKV[0][sti], b_QKV[1][sti], b_QKV[2][sti]
                    trib = tri_bf[:, d, :]
                    ghl, gl32 = ts["ghl"], ts["gl32"]
                    P.op("dve", lambda e: e.tensor_copy(out=ghl[:, 0:1], in_=gcol), rd=[b_GB[ti]], wr=[ts["b_ghl"]])
                    P.op("dve", lambda e: e.tensor_tensor(out=gl32[:, 0:1], in0=gcol, in1=ghl[:, 0:1], op=ALU.subtract),
                         rd=[b_GB[ti], ts["b_ghl"]], wr=[ts["b_gl32"]])
                    P.op("dve", lambda e: e.tensor_copy(out=ghl[:, 1:2], in_=gl32[:, 0:1]), rd=[ts["b_gl32"]],
                         wr=[ts["b_ghl"]])
                    P.op("dve", lambda e: e.tensor_copy(out=gl32[:, 1:2], in_=ghl[:, 0:1]), rd=[ts["b_ghl"]],
                         wr=[ts["b_gl32"]])
                    P.op("dve", lambda e: e.tensor_scalar(out=ts["gbc"][:, 0, :], in0=one_bf[:], scalar1=gl32[:, 1:2],
                                                          scalar2=None, op0=ALU.mult),
                         rd=[b_w, ts["b_gl32"]], wr=[ts["b_gbc"]])
                    P.op("dve", lambda e: e.tensor_scalar(out=ts["gbc"][:, 1, :], in0=one_bf[:], scalar1=gl32[:, 0:1],
                                                          scalar2=None, op0=ALU.mult),
                         rd=[b_w, ts["b_gl32"]], wr=[ts["b_gbc"]])
                    yield
                    pr, prb = nq(bank)
                    pgq, pgb2 = nq(bank)
                    P.mm([lambda e: e.matmul(pr, ts["gbc"][:, 0, :], trib, start=True, stop=False),
                          lambda e: e.matmul(pr, ts["gbc"][:, 1, :], trib, start=False, stop=True)],
                         rd=[ts["b_gbc"], b_w], wr=[prb])
                    P.mm([lambda e: e.matmul(pgq[:, 0:2], trib, ghl[:, 0:2], start=True, stop=True),
                          lambda e: e.matmul(pgq[:, 2:4], one_bf[:], ghl[:, 0:2], start=True, stop=True)],
                         rd=[b_w, ts["b_ghl"]], wr=[pgb2])
                    yield
                    P.op("dve", lambda e: e.tensor_tensor(out=sc[:, 0:1], in0=pgq[:, 0:1], in1=pgq[:, 1:2], op=ALU.add)
                         if False else e.tensor_reduce(out=sc[:, 0:2], in_=pgq[:, 0:4].rearrange("p (a b) -> p a b", b=2),
                                                       axis=AX.X, op=ALU.add), rd=[pgb2], wr=[ts["b_sc"]])
                    P.op("dve", lambda e: e.tensor_tensor(out=sc[:, 2:3], in0=sc[:, 1:2], in1=sc[:, 0:1],
                                                          op=ALU.subtract), rd=[ts["b_sc"]], wr=[ts["b_sc"]])
                    P.op("dve", lambda e: e.tensor_scalar(out=sc[:, 3:4], in0=sc[:, 0:1], scalar1=-1.0, scalar2=None,
                                                          op0=ALU.mult), rd=[ts["b_sc"]], wr=[ts["b_sc"]])
                    P.op("act", lambda e: e.activation(out=sc[:, 4:7], in_=sc[:, 0:3], func=AF.Exp),
                         rd=[ts["b_sc"]], wr=[ts["b_sc"]])
                    P.op("dve", lambda e: e.tensor_tensor(out=sc[:, 7:8], in0=sc[:, 4:5], in1=bcol, op=ALU.mult),
                         rd=[ts["b_sc"], b_GB[ti]], wr=[ts["b_sc"]])
                    P.op("act", lambda e: e.activation(out=sl["a"][:, 0:1], in_=sc[:, 5:6], func=AF.Copy),
                         rd=[ts["b_sc"]], wr=[sl["b_a"]])
                    yield
                    P.op("dve", lambda e: e.scalar_tensor_tensor(out=ts["t1"][:], in0=pr, scalar=-1.0, in1=nml,
                                                                 op0=ALU.mult, op1=ALU.add),
                         rd=[prb, b_w], wr=[ts["b_t1"]])
                    P.op("act", lambda e: e.activation(out=ts["D"][:], in_=ts["t1"][:], func=AF.Exp, bias=sc[:, 0:1],
                                                       scale=1.0), rd=[ts["b_t1"], ts["b_sc"]], wr=[ts["b_D"]])
                    yield
                    P.op("dve", lambda e: e.tensor_tensor(out=ts["t2"][:], in0=pr, in1=nmu, op=ALU.add),
                         rd=[prb, b_w], wr=[ts["b_t2"]])
                    P.op("act", lambda e: e.activation(out=ts["DT"][:], in_=ts["t2"][:], func=AF.Exp, bias=sc[:, 3:4],
                                                       scale=1.0), rd=[ts["b_t2"], ts["b_sc"]], wr=[ts["b_DT"]])
                    P.op("act", lambda e: e.activation(out=ts["egc"][:], in_=pr, func=AF.Exp), rd=[prb],
                         wr=[ts["b_egc"]])
                    yield
                    pgm, pgmb = nq(bank)
                    ps2, ps2b = nq(bank)
                    P.mm([lambda e: e.matmul(pgm, kT, kT, start=True, stop=True)], rd=[bk], wr=[pgmb])
                    P.mm([lambda e: e.matmul(ps2, kT, qTt, start=True, stop=True)], rd=[bk, bq], wr=[ps2b])
                    yield
                    P.op("dve", lambda e: e.scalar_tensor_tensor(out=ts["L"][:], in0=pgm, scalar=bcol, in1=ts["D"][:],
                                                                 op0=ALU.mult, op1=ALU.mult),
                         rd=[pgmb, b_GB[ti], ts["b_D"]], wr=[ts["b_L"]])
                    P.op("dve", lambda e: e.tensor_tensor(out=sl["qkT"][:], in0=ps2, in1=ts["DT"][:], op=ALU.mult),
                         rd=[ps2b, ts["b_DT"]], wr=[sl["b_qkT"]])
                    P.op("pool", lambda e: e.tensor_tensor(out=sl["qgT"][:], in0=qTt, in1=ts["egc"][:], op=ALU.mult),
                         rd=[bq, ts["b_egc"]], wr=[sl["b_qgT"]])
                    yield
                    P.op("pool", lambda e: e.tensor_tensor(
                        out=ts["Call"][:], in0=ts["L"][:].unsqueeze(1).to_broadcast([128, 7, 128]), in1=cma,
                        op=ALU.mult), rd=[ts["b_L"], b_w], wr=[ts["b_Call"]])
                    yield
                    ptb = self.PT if d == 0 else self.PT2
                    P.mm([(lambda e, lv=lv: e.transpose(ptb[:, lv * 128:(lv + 1) * 128], ts["Call"][:, lv, :], ident[:]))
                          for lv in range(7)], rd=[ts["b_Call"], self.CONSTb], wr=[b_PT[d]])
                    yield
                    P.op("dve", lambda e: e.tensor_copy(out=ts["CTall"][:],
                                                        in_=ptb[:, 0:896].rearrange("p (l t) -> p l t", l=7)),
                         rd=[b_PT[d]], wr=[ts["b_CTall"]])
                    yield
                    P.op("dve", lambda e: e.tensor_tensor(out=ts["T32"][:, 0, :], in0=CST[:, C_ID:C_ID + 128],
                                                          in1=ts["Call"][:, 0, :], op=ALU.subtract),
                         rd=[self.CONSTb, ts["b_Call"]], wr=[ts["b_T"][0]])
                    P.op("pool", lambda e: e.tensor_tensor(out=ts["TT32"][:, 0, :], in0=CST[:, C_ID:C_ID + 128],
                                                           in1=ts["CTall"][:, 0, :], op=ALU.subtract),
                         rd=[self.CONSTb, ts["b_CTall"]], wr=[ts["b_TT"][0]])
                    P.op("act", lambda e: e.activation(out=ts["Tbf"][:, 0, :], in_=ts["T32"][:, 0, :], func=AF.Copy),
                         rd=[ts["b_T"][0]], wr=[ts["b_T"][0]])
                    P.op("act", lambda e: e.activation(out=ts["TTbf"][:, 0, :], in_=ts["TT32"][:, 0, :], func=AF.Copy),
                         rd=[ts["b_TT"][0]], wr=[ts["b_TT"][0]])
                    yield
                    cur = 0
                    for lv in range(1, 7):
                        nxt = 1 - cur
                        lastl = (lv == 6)
                        pw, pwb = nq(bank)
                        pw2, pw2b = nq(bank)
                        if not lastl:
                            P.mm([lambda e, lv=lv, cur=cur: e.matmul(pw, ts["CTall"][:, lv, :], ts["Tbf"][:, cur, :],
                                                                     start=True, stop=True)],
                                 rd=[ts["b_CTall"], ts["b_T"][cur]], wr=[pwb])
                        P.mm([lambda e, lv=lv, cur=cur: e.matmul(pw2, ts["Call"][:, lv, :], ts["TTbf"][:, cur, :],
                                                                 start=True, stop=True)],
                             rd=[ts["b_Call"], ts["b_TT"][cur]], wr=[pw2b])
                        yield
                        if not lastl:
                            P.op("act", lambda e: e.activation(out=ts["W"][:, 0, :], in_=pw, func=AF.Copy),
                                 rd=[pwb], wr=[ts["b_W"][0]])
                        P.op("act", lambda e: e.activation(out=ts["W"][:, 1, :], in_=pw2, func=AF.Copy),
                             rd=[pw2b], wr=[ts["b_W"][1]])
                        yield
                        pu, pub = nq(bank)
                        pu2, pu2b = nq(bank)
                        if not lastl:
                            P.mm([lambda e, cur=cur: e.matmul(pu, ts["TTbf"][:, cur, :], ts["W"][:, 0, :],
                                                              start=True, stop=True)],
                                 rd=[ts["b_TT"][cur], ts["b_W"][0]], wr=[pub])
                        P.mm([lambda e, cur=cur: e.matmul(pu2, ts["Tbf"][:, cur, :], ts["W"][:, 1, :],
                                                          start=True, stop=True)],
                             rd=[ts["b_T"][cur], ts["b_W"][1]], wr=[pu2b])
                        yield
                        if not lastl:
                            P.op("dve", lambda e, cur=cur, nxt=nxt: e.tensor_tensor(
                                out=ts["T32"][:, nxt, :], in0=ts["T32"][:, cur, :], in1=pu, op=ALU.subtract),
                                rd=[ts["b_T"][cur], pub], wr=[ts["b_T"][nxt]])
                            P.op("act", lambda e, nxt=nxt: e.activation(out=ts["Tbf"][:, nxt, :],
                                                                        in_=ts["T32"][:, nxt, :], func=AF.Copy),
                                 rd=[ts["b_T"][nxt]], wr=[ts["b_T"][nxt]])
                            P.op("dve", lambda e, cur=cur, nxt=nxt: e.tensor_tensor(
                                out=ts["TT32"][:, nxt, :], in0=ts["TT32"][:, cur, :], in1=pu2, op=ALU.subtract),
                                rd=[ts["b_TT"][cur], pu2b], wr=[ts["b_TT"][nxt]])
                            P.op("act", lambda e, nxt=nxt: e.activation(out=ts["TTbf"][:, nxt, :],
                                                                        in_=ts["TT32"][:, nxt, :], func=AF.Copy),
                                 rd=[ts["b_TT"][nxt]], wr=[ts["b_TT"][nxt]])
                        else:
                            P.op("dve", lambda e, cur=cur: e.tensor_tensor(
                                out=sl["TT"][:], in0=ts["TT32"][:, cur, :], in1=pu2, op=ALU.subtract),
                                rd=[ts["b_TT"][cur], pu2b], wr=[sl["b_TT"]])
                        yield
                        cur = nxt
                    ptv = ptb[:, 896:1024]
                    P.mm([lambda e: e.transpose(ptv, kT, ident[:])], rd=[bk, self.CONSTb], wr=[b_PT[2 + d]])
                    yield
                    P.op("dve", lambda e: e.tensor_scalar(out=sl["kbg"][:], in0=ptv, scalar1=sc[:, 7:8],
                                                          scalar2=None, op0=ALU.mult),
                         rd=[b_PT[2 + d], ts["b_sc"]], wr=[sl["b_kbg"]])
                    P.op("dve", lambda e: e.tensor_scalar(out=sl["kd"][:], in0=ptv, scalar1=sc[:, 6:7],
                                                          scalar2=None, op0=ALU.mult),
                         rd=[b_PT[2 + d], ts["b_sc"]], wr=[sl["b_kd"]])
                    yield
                    P.mm([lambda e: e.transpose(ptv, vT, ident[:])], rd=[bv, self.CONSTb], wr=[b_PT[2 + d]])
                    yield
                    P.op("dve", lambda e: e.tensor_scalar(out=sl["vb"][:], in0=ptv, scalar1=bcol,
                                                          scalar2=None, op0=ALU.mult),
                         rd=[b_PT[2 + d], b_GB[ti]], wr=[sl["b_vb"]])
                    yield

                def gen_R(d, ti, sl, si):
                    rs_ = RS[d]
                    bank = 2 + d
                    cur, nxt = si % 2, (si + 1) % 2
                    pwt, pwtb = nq(bank)
                    P.mm([lambda e: e.matmul(pwt, sl["kbg"][:], sl["TT"][:], start=True, stop=True)],
                         rd=[sl["b_kbg"], sl["b_TT"]], wr=[pwtb])
                    yield
                    P.op("dve", lambda e: e.tensor_scalar(out=rs_["nwT"][:], in0=pwt, scalar1=-1.0, scalar2=None,
                                                          op0=ALU.mult), rd=[pwtb], wr=[rs_["b_nwT"]])
                    yield
                    pvn, pvnb = nq(bank)
                    P.mm([lambda e: e.matmul(pvn, sl["TT"][:], sl["vb"][:], start=True, stop=False),
                          lambda e: e.matmul(pvn, rs_["nwT"][:], rs_["Sbf"][:, cur, :], start=False, stop=True)],
                         rd=[sl["b_TT"], sl["b_vb"], rs_["b_nwT"], rs_["b_Sbf"][cur]], wr=[pvnb])
                    yield
                    P.op("act", lambda e: e.activation(out=rs_["vnew"][:], in_=pvn, func=AF.Copy),
                         rd=[pvnb], wr=[rs_["b_vnew"]])
                    yield
                    po, pob = nq(bank)
                    psn, psnb = nq(bank)
                    P.mm([lambda e: e.matmul(po, sl["qkT"][:], rs_["vnew"][:], start=True, stop=False),
                          lambda e: e.matmul(po, sl["qgT"][:], rs_["Sbf"][:, cur, :], start=False, stop=True)],
                         rd=[sl["b_qkT"], rs_["b_vnew"], sl["b_qgT"], rs_["b_Sbf"][cur]], wr=[pob])
                    P.mm([lambda e: e.matmul(psn, sl["kd"][:], rs_["vnew"][:], start=True, stop=True)],
                         rd=[sl["b_kd"], rs_["b_vnew"]], wr=[psnb])
                    yield
                    P.op("dve", lambda e: e.scalar_tensor_tensor(out=rs_["S32"][:], in0=rs_["S32"][:], scalar=sl["a"][:, 0:1],
                                                                 in1=psn, op0=ALU.mult, op1=ALU.add),
                         rd=[rs_["b_S32"], sl["b_a"], psnb], wr=[rs_["b_S32"]])
                    P.op("act", lambda e: e.activation(out=Od[:, d, ti, :], in_=po, func=AF.Copy), rd=[pob],
                         wr=[b_O[d][ti]])
                    yield
                    P.op("act", lambda e: e.activation(out=rs_["Sbf"][:, nxt, :], in_=rs_["S32"][:], func=AF.Copy),
                         rd=[rs_["b_S32"]], wr=[rs_["b_Sbf"][nxt]])
                    yield

                def run_streams(gens):
                    gens = list(gens)
                    while gens:
                        for g in list(gens):
                            try:
                                next(g)
                            except StopIteration:
                                gens.remove(g)

                orders = [list(range(18)), [1, 0] + list(range(17, 1, -1))]
                for d in range(2):
                    P.op("dve", lambda e, d=d: e.memset(RS[d]["S32"][:], 0.0), wr=[RS[d]["b_S32"]])
                    P.op("pool", lambda e, d=d: e.memset(RS[d]["Sbf"][:, 0, :], 0.0), wr=[RS[d]["b_Sbf"][0]])
                TSTOP = int(_os.environ.get("DN_TSTOP", "0"))
                if TSTOP:
                    for d in range(2):
                        g_ = gen_T(d, orders[d][0], SL[d][0])
                        for _ in range(TSTOP):
                            next(g_)
                else:
                    run_streams([gen_T(d, orders[d][0], SL[d][0]) for d in range(2)])
                if STOP == 3:
                    P.barrier()
                    return
                NST = int(_os.environ.get('DN_STEPS', '18'))
                for si in range(NST):
                    gens = [gen_R(d, orders[d][si], SL[d][si % 2], si) for d in range(2)]
                    RSTOP = int(_os.environ.get("DN_RSTOP", "0"))
                    if RSTOP:
                        for g_ in gens:
                            for _ in range(RSTOP):
                                next(g_)
                        continue
                    if si + 1 < NST:
                        gens += [gen_T(d, orders[d][si + 1], SL[d][(si + 1) % 2]) for d in range(2)]
                    run_streams(gens)

                if STOP == 4:
                    P.barrier()
                    return
                for ti in range(18):
                    t0 = ti * 128
                    sti = 0 if ti < 2 else 1 + (ti - 2) // 4
                    c0 = t0 - TILES[sti][0]
                    pz, pzb = PS[5][:, 0:128], psq[5][0]
                    P.mm([(lambda e, kc=kc: e.matmul(pz, self.HT[:, kc, t0:t0 + 128], wz[:, kc, :],
                                                     start=(kc == 0), stop=(kc == NCH - 1))) for kc in range(NCH)],
                         rd=[b_wz, self.HTb[sti]], wr=[pzb])
                    P.op("act", lambda e: e.activation(out=zs[:], in_=pz, func=AF.Silu), rd=[pzb], wr=[b_zs])
                    P.op("dve", lambda e: e.tensor_tensor(out=osum[:], in0=Od[:, 0, ti, :], in1=Od[:, 1, ti, :],
                                                          op=ALU.add), rd=[b_O[0][ti], b_O[1][ti]], wr=[b_osum])
                    P.op("pool", lambda e: e.tensor_tensor(out=osq[:], in0=osum[:], in1=osum[:], op=ALU.mult),
                         rd=[b_osum], wr=[b_osq])
                    P.op("dve", lambda e: e.tensor_reduce(out=ost[:, 0:1], in_=osq[:], axis=AX.X, op=ALU.add),
                         rd=[b_osq], wr=[b_ost])
                    P.op("dve", lambda e: e.tensor_scalar(out=ost[:, 1:2], in0=ost[:, 0:1], scalar1=1.0 / 128.0,
                                                          scalar2=EPS, op0=ALU.mult, op1=ALU.add),
                         rd=[b_ost], wr=[b_ost])
                    P.op("pool", lambda e: e.tensor_tensor(out=ost[:, 2:3], in0=ost[:, 1:2], in1=CST[:, C_NH:C_NH + 1],
                                                           op=ALU.pow), rd=[b_ost, self.CONSTb], wr=[b_ost])
                    P.op("dve", lambda e: e.scalar_tensor_tensor(out=osum[:], in0=osum[:], scalar=ost[:, 2:3],
                                                                 in1=ogain[:], op0=ALU.mult, op1=ALU.mult),
                         rd=[b_osum, b_ost, b_w], wr=[b_osum])
                    P.op("pool", lambda e: e.tensor_tensor(out=ybf[:], in0=osum[:], in1=zs[:], op=ALU.mult),
                         rd=[b_osum, b_zs], wr=[b_y])
                    ptc = self.PT[:, 896:1024]
                    P.mm([lambda e: e.transpose(ptc, ybf[:], ident[:])], rd=[b_y, self.CONSTb], wr=[b_PT[2]])
                    P.op("act", lambda e: e.activation(out=yT[:, c0:c0 + 128], in_=ptc, func=AF.Copy),
                         rd=[b_PT[2]], wr=[b_yT])
                    if c0 + 128 == TILES[sti][1]:
                        n = TILES[sti][1]
                        self.res_add(l, s, sti, xt, b_xt, 1, lambda fc, c: woh[:, c * 128:(c + 1) * 128],
                                     lambda fc: yT[:, :n], [b_woh, b_yT])
            P.barrier()

    def build(self, plan):
        nc, P = self.nc, self.P
        self.prologue()
        for s in range(self.nseq):
            groups, cur = [], []
            for (kind, l, i) in plan:
                if kind == "ffn":
                    cur.append(("ffn", l, i))
                else:
                    cur.append(("norm", l, 1))
                    groups.append(("x", cur))
                    cur = []
                    groups.append(("mix", l))
            groups.append(("x", cur))
            first = True
            for gi, g in enumerate(groups):
                if g[0] == "x":
                    with nc.sbuf_tensor(self.U("XT"), [128, NCH, T], F32) as XT:
                        self.XT = XT
                        self.XTb = [P.buf(f"XT{i}") for i in range(len(TILES))]
                        for ti, (t0, n, isc) in enumerate(TILES):
                            if first:
                                P.dma("sp", XT[:, :, t0:t0 + n], self.xT[s, :, :, t0:t0 + n], wr=[self.XTb[ti]])
                            else:
                                P.dma("sp", XT[:, :, t0:t0 + n], self.xs[:, :, t0:t0 + n], rd=[self.XSb[ti]],
                                      wr=[self.XTb[ti]])
                        for (kind, l, i) in g[1]:
                            if kind == "ffn":
                                self.ffn(l, i, s)
                            else:
                                self._norm_only(l, 1, s)
                        if gi == len(groups) - 1:
                            self.store_x(s)
                        else:
                            for ti, (t0, n, isc) in enumerate(TILES):
                                P.dma("sp", self.xs[:, :, t0:t0 + n], XT[:, :, t0:t0 + n], rd=[self.XTb[ti]],
                                      wr=[self.XSb[ti]])
                        P.barrier()
                    self.XT = None
                    first = False
                else:
                    l = g[1]
                    if l % 3 == 2:
                        self.mix_ret(l, s)
                    elif l % 3 == 1:
                        self.mix_swa(l, s)
                    else:
                        self.mix_dn(l, s)
        P.finish()
        return nc


def _blk(W, c0, ncol):
    return np.ascontiguousarray(W[:, c0:c0 + ncol].reshape(NCH, 128, ncol).transpose(1, 0, 2))


def _prep_shared(inp, layers):
    f = np.float32
    L = list(layers)
    ada_w = np.ascontiguousarray(
        inp["ada_w"][L].astype(f, copy=False).reshape(len(L), NCH, 128, 72, 128).transpose(0, 3, 2, 1, 4))
    ada_b = np.ascontiguousarray(inp["ada_b"][L].astype(f, copy=False).reshape(len(L), 72, 128).transpose(0, 2, 1))
    norm_g = np.ascontiguousarray(inp["norm_g"][L].astype(f, copy=False).reshape(len(L), 24, 128).transpose(0, 2, 1))
    w1 = np.ascontiguousarray(
        inp["ffn_w1"][L].astype(f, copy=False).reshape(len(L), 2, NCH, 128, 44, 128).transpose(0, 1, 4, 3, 2, 5))
    w2 = np.ascontiguousarray(
        inp["ffn_w2"][L].astype(f, copy=False).reshape(len(L), 2, NFB, 128, NCH, 128).transpose(0, 1, 4, 3, 2, 5))
    out = {"ada_w": ada_w, "ada_b": ada_b, "norm_g": norm_g, "ffn_w1": w1, "ffn_w2": w2, "cst": _make_consts()}
    if 0 in L or 3 in L:
        wqkv = np.zeros((2, 24, 128, NCH, 128), f)
        wz = np.zeros((2, 8, 128, NCH, 128), f)
        wab = np.zeros((2, 128, NCH, 32), f)
        for sl in range(2):
            W = inp["dn_w_in"][sl].astype(f, copy=False)
            for blk in range(24):
                wqkv[sl, blk] = _blk(W, blk * 128, 128)
            for h in range(8):
                wz[sl, h] = _blk(W, 3072 + h * 128, 128)
            wab[sl] = _blk(W, 4096, 32)
        out["dn_wqkv"], out["dn_wz"], out["dn_wab"] = wqkv, wz, wab
        out["dn_wo"] = np.ascontiguousarray(inp["dn_w_out"].astype(f, copy=False).reshape(2, 8, 128, D))
        out["dn_cw"] = np.ascontiguousarray(inp["dn_conv"].astype(f, copy=False).reshape(2, 5, 24, 128).transpose(0, 3, 2, 1))
        ab = np.stack([inp["dn_a_log"].reshape(2, 16), inp["dn_dt_bias"].reshape(2, 16)], axis=1).astype(f)
        out["dn_ab"] = np.ascontiguousarray(np.broadcast_to(ab[:, None], (2, 128, 2, 16)))
        out["dn_og"] = np.ascontiguousarray(np.broadcast_to(inp["dn_o_gain"].astype(f)[:, None, :], (2, 128, 128)))
        out["dn_masks"] = _make_dn_masks()
    if 1 in L:
        out["swa_wqkv"] = _blk(inp["swa_w_qkv"][0].astype(f, copy=False), 0, 1536)
        out["swa_wo"] = _blk(inp["swa_w_out"][0].astype(f, copy=False), 0, D)
        g = np.stack([inp["swa_q_gain"][0], inp["swa_k_gain"][0]]).astype(f)
        out["swa_g"] = np.ascontiguousarray(np.broadcast_to(g[None], (128, 2, 64)))
        out["swa_sink"] = np.ascontiguousarray(np.broadcast_to(inp["swa_sink"][0].astype(f)[None], (128, 16)))
        inv = (np.float32(10000.0) ** (-np.arange(16, dtype=f) / np.float32(16))).astype(f)
        rows = np.repeat(np.arange(SEQ // 64), 64).astype(f)
        cols = np.tile(np.arange(64), SEQ // 64).astype(f)
        ang = np.concatenate([rows[:, None] * inv[None, :], cols[:, None] * inv[None, :]], axis=-1).astype(f)
        out["swa_cs"] = np.ascontiguousarray(np.stack([np.cos(ang), np.sin(ang)], axis=1).astype(f))
    if 2 in L:
        W = inp["ret_w_in"][0].astype(f, copy=False)
        wqk = np.zeros((4, 4, 128, NCH, 128), f)
        wvg = np.zeros((4, 3, 128, NCH, 512), f)
        for h in range(4):
            for blk in range(4):
                c0 = (0 if blk < 2 else 1024) + h * 256 + (blk % 2) * 128
                wqk[h, blk] = _blk(W, c0, 128)
            for k in range(3):
                wvg[h, k] = _blk(W, 2048 * (k + 1) + h * 512, 512)
        out["ret_wqk"] = wqk
        out["ret_wvg"] = wvg
        out["ret_wo"] = np.ascontiguousarray(
            inp["ret_w_out"][0].astype(f, copy=False).reshape(4, 4, 128, D).transpose(0, 2, 1, 3))
        out["ret_gain"] = np.ascontiguousarray(
            np.broadcast_to(inp["ret_gn_gain"][0].astype(f, copy=False)[None], (128, 2, 2048)))
        inv = (np.float32(10000.0) ** (-np.arange(128, dtype=f) / np.float32(128))).astype(f)
        ang = (np.arange(SEQ, dtype=f)[:, None] * inv[None, :]).astype(f)
        out["ret_cs"] = np.ascontiguousarray(np.stack([np.cos(ang).T, np.sin(ang).T]).astype(f))
        out["ret_c"] = _make_ret_consts()
    return out


def _prep_core(inp, b0, nseq):
    f = np.float32
    x = inp["x"][b0:b0 + nseq]
    ctx = inp["ctx"][b0:b0 + nseq]
    full = np.concatenate([ctx, x], axis=1)
    xT = np.ascontiguousarray(full.reshape(nseq, T, NCH, 128).transpose(0, 3, 2, 1)).astype(f, copy=False)
    cc = np.zeros((5, D), f)
    cc[:nseq] = inp["c"][b0:b0 + nseq]
    cc[4] = inp["c_ctx"]
    cT = np.ascontiguousarray(cc.reshape(5, NCH, 128).transpose(2, 1, 0))
    return {"xT": xT, "cT": cT}


FULL_PLAN = []
for _l in range(DEPTH):
    FULL_PLAN += [("ffn", _l, 0), ("mix", _l, 0), ("ffn", _l, 1)]


def run(inp, nseq, plan, ncores=NCORES, trace=False):
    import time as _t
    _t0 = _t.time()
    layers = sorted({l for (_, l, _) in plan})
    bld = Builder(nseq, layers)
    nc = bld.build(plan)
    print("[kernel] build", round(_t.time() - _t0, 1), flush=True)
    shared = _prep_shared(inp, layers)
    in_maps = []
    for c in range(ncores):
        m = dict(shared)
        m.update(_prep_core(inp, c * nseq, nseq))
        in_maps.append(m)
    print("[kernel] prep", round(_t.time() - _t0, 1), flush=True)
    res = run_bass_kernel_spmd(nc, in_maps, core_ids=list(range(ncores)), trace=trace)
    print("[kernel] ran", round(_t.time() - _t0, 1), flush=True)
    outs = []
    for c in range(ncores):
        oT = res.results[c]["outT"]
        outs.append(np.ascontiguousarray(oT.transpose(0, 3, 2, 1)).reshape(nseq, SEQ, D))
    return np.concatenate(outs, axis=0), res


def kernel(**inputs):
    out, _ = run(inputs, 4, FULL_PLAN)
    return out.astype(np.float32, copy=False)
```
